# Optimizing a Trainium2 kernel written in Bass

```python
import math
import jax, jax.numpy as jnp
from jax import lax
import numpy as np

D_MODEL = 1024
BATCH = 8
SEQ = 4096
DEPTH = 1

HEAD_DIM = 64
FOX_HEADS = D_MODEL // (2 * HEAD_DIM)
FOX_WIDTH = FOX_HEADS * HEAD_DIM
MLA_HEADS = D_MODEL // (2 * HEAD_DIM)
MLA_NOPE_DIM = 64
MLA_ROPE_DIM = 32
MLA_QK_DIM = MLA_NOPE_DIM + MLA_ROPE_DIM
MLA_V_DIM = 64
MLA_WIDTH = MLA_HEADS * MLA_V_DIM
MIX_WIDTH = FOX_WIDTH + MLA_WIDTH
Q_LORA_RANK = 3 * D_MODEL // 8
KV_LORA_RANK = D_MODEL // 4
D_FF = 4 * D_MODEL
BLOCK_Q = 128
ROPE_THETA = 10000.0
EPS = 1e-6

OFF_FQ = 0
OFF_FK = OFF_FQ + FOX_WIDTH
OFF_FV = OFF_FK + FOX_WIDTH
OFF_FF = OFF_FV + FOX_WIDTH
OFF_CQ = OFF_FF + FOX_HEADS
OFF_CKV = OFF_CQ + Q_LORA_RANK
OFF_KR = OFF_CKV + KV_LORA_RANK
IN_COLS = OFF_KR + MLA_ROPE_DIM

kernel_name = "hymba_fox_mla_hybrid_block"


def rmsnorm(x, g):
    x32 = x.astype(jnp.float32)
    y = x32 * lax.rsqrt(jnp.mean(x32 * x32, axis=-1, keepdims=True) + EPS)
    return (y * g.astype(jnp.float32)).astype(x.dtype)


def rope_cos_sin(positions):
    inv_freq = ROPE_THETA ** (-jnp.arange(0, MLA_ROPE_DIM, 2, dtype=jnp.float32) / MLA_ROPE_DIM)
    ang = positions.astype(jnp.float32)[..., None] * inv_freq
    return jnp.cos(ang), jnp.sin(ang)


def apply_rope(x, cos, sin):
    half = x.shape[-1] // 2
    x1 = x[..., :half].astype(jnp.float32)
    x2 = x[..., half:].astype(jnp.float32)
    return jnp.concatenate([x1 * cos - x2 * sin, x2 * cos + x1 * sin], axis=-1).astype(x.dtype)


def causal_block_attention(q, k, v, scale, log_decay_cum=None):
    b, h, s, dk = q.shape
    nb = s // BLOCK_Q
    q_blocks = jnp.moveaxis(q.reshape(b, h, nb, BLOCK_Q, dk), 2, 0)
    starts = jnp.arange(nb, dtype=jnp.int32) * BLOCK_Q
    key_pos = jnp.arange(s, dtype=jnp.int32)

    def attend(q_blk, start, bias):
        logits = jnp.einsum('bhqd,bhkd->bhqk', q_blk, k).astype(jnp.float32) * scale
        if bias is not None:
            logits = logits + bias
        q_pos = start + jnp.arange(BLOCK_Q, dtype=jnp.int32)
        mask = key_pos[None, :] <= q_pos[:, None]
        logits = jnp.where(mask, logits, -jnp.inf)
        p = jax.nn.softmax(logits, axis=-1).astype(v.dtype)
        return jnp.einsum('bhqk,bhkd->bhqd', p, v)

    if log_decay_cum is None:
        out = lax.map(lambda xs: attend(xs[0], xs[1], None), (q_blocks, starts))
    else:
        F = log_decay_cum
        fq_blocks = jnp.moveaxis(F.reshape(b, h, nb, BLOCK_Q), 2, 0)
        out = lax.map(
            lambda xs: attend(xs[0], xs[1], xs[2][..., :, None] - F[:, :, None, :]),
            (q_blocks, starts, fq_blocks))
    return jnp.moveaxis(out, 0, 2).reshape(b, h, s, -1)


def setup_inputs(seed: int = 0) -> dict:
    key = jax.random.key(seed)
    ks = jax.random.split(key, 16)
    f32 = jnp.float32

    def w(k, shape, fan_in):
        return jax.random.normal(k, shape, f32) * (fan_in ** -0.5)

    def gain(k, shape):
        return 1.0 + 0.02 * jax.random.normal(k, shape, f32)

    x = jax.random.normal(ks[0], (BATCH, SEQ, D_MODEL), f32)
    offsets = jax.random.randint(ks[1], (BATCH, 1), 0, 64, dtype=jnp.int32)
    positions = jnp.arange(SEQ, dtype=jnp.int32)[None, :] + offsets
    return {
        "x": x,
        "positions": positions,
        "attn_norm_g": gain(ks[2], (DEPTH, D_MODEL)),
        "w_in": w(ks[3], (DEPTH, D_MODEL, IN_COLS), D_MODEL),
        "b_forget": 3.0 + 0.5 * jax.random.normal(ks[4], (DEPTH, FOX_HEADS), f32),
        "q_norm_g": gain(ks[5], (DEPTH, Q_LORA_RANK)),
        "w_uq": w(ks[6], (DEPTH, Q_LORA_RANK, MLA_HEADS * MLA_QK_DIM), Q_LORA_RANK),
        "kv_norm_g": gain(ks[7], (DEPTH, KV_LORA_RANK)),
        "w_ukv": w(ks[8], (DEPTH, KV_LORA_RANK, MLA_HEADS * (MLA_NOPE_DIM + MLA_V_DIM)), KV_LORA_RANK),
        "fox_out_g": gain(ks[9], (DEPTH, FOX_WIDTH)),
        "mla_out_g": gain(ks[10], (DEPTH, MLA_WIDTH)),
        "w_o": w(ks[11], (DEPTH, MIX_WIDTH, D_MODEL), MIX_WIDTH),
        "mlp_norm_g": gain(ks[12], (DEPTH, D_MODEL)),
        "w_up": w(ks[13], (DEPTH, D_MODEL, D_FF), D_MODEL),
        "w_down": w(ks[14], (DEPTH, D_FF, D_MODEL), D_FF),
        "final_norm_g": gain(ks[15], (D_MODEL,)),
    }


def reference(x, positions, attn_norm_g, w_in, b_forget, q_norm_g, w_uq, kv_norm_g, w_ukv,
              fox_out_g, mla_out_g, w_o, mlp_norm_g, w_up, w_down, final_norm_g):
    b, s, _ = x.shape
    cos, sin = rope_cos_sin(positions)
    fox_scale = 1.0 / math.sqrt(HEAD_DIM)
    mla_scale = 1.0 / math.sqrt(MLA_QK_DIM)

    for l in range(DEPTH):
        h = rmsnorm(x, attn_norm_g[l])
        proj = jnp.einsum('bsd,dc->bsc', h, w_in[l])

        fq = proj[..., OFF_FQ:OFF_FK].reshape(b, s, FOX_HEADS, HEAD_DIM).transpose(0, 2, 1, 3)
        fk = proj[..., OFF_FK:OFF_FV].reshape(b, s, FOX_HEADS, HEAD_DIM).transpose(0, 2, 1, 3)
        fv = proj[..., OFF_FV:OFF_FF].reshape(b, s, FOX_HEADS, HEAD_DIM).transpose(0, 2, 1, 3)
        f_logit = proj[..., OFF_FF:OFF_CQ].astype(jnp.float32) + b_forget[l].astype(jnp.float32)
        log_f = jax.nn.log_sigmoid(f_logit)
        F = jnp.cumsum(log_f, axis=1).transpose(0, 2, 1)
        fox = causal_block_attention(fq, fk, fv, fox_scale, F)
        fox = fox.transpose(0, 2, 1, 3).reshape(b, s, FOX_WIDTH)

        c_q = rmsnorm(proj[..., OFF_CQ:OFF_CKV], q_norm_g[l])
        c_kv = rmsnorm(proj[..., OFF_CKV:OFF_KR], kv_norm_g[l])
        k_rope = apply_rope(proj[..., OFF_KR:IN_COLS], cos, sin)
        q = jnp.einsum('bsr,rc->bsc', c_q, w_uq[l]).reshape(b, s, MLA_HEADS, MLA_QK_DIM)
        q_nope = q[..., :MLA_NOPE_DIM]
        q_rope = apply_rope(q[..., MLA_NOPE_DIM:], cos[:, :, None, :], sin[:, :, None, :])
        kv = jnp.einsum('bsr,rc->bsc', c_kv, w_ukv[l]).reshape(b, s, MLA_HEADS, MLA_NOPE_DIM + MLA_V_DIM)
        k_nope = kv[..., :MLA_NOPE_DIM]
        mv = kv[..., MLA_NOPE_DIM:]
        mq = jnp.concatenate([q_nope, q_rope], axis=-1).transpose(0, 2, 1, 3)
        mk = jnp.concatenate(
            [k_nope, jnp.broadcast_to(k_rope[:, :, None, :], (b, s, MLA_HEADS, MLA_ROPE_DIM))],
            axis=-1).transpose(0, 2, 1, 3)
        mv = mv.transpose(0, 2, 1, 3)
        mla = causal_block_attention(mq, mk, mv, mla_scale)
        mla = mla.transpose(0, 2, 1, 3).reshape(b, s, MLA_WIDTH)

        mixed = jnp.concatenate([rmsnorm(fox, fox_out_g[l]), rmsnorm(mla, mla_out_g[l])], axis=-1)
        x = x + jnp.einsum('bsc,cd->bsd', mixed, w_o[l])

        h = rmsnorm(x, mlp_norm_g[l])
        u = jnp.einsum('bsd,df->bsf', h, w_up[l])
        x = x + jnp.einsum('bsf,fd->bsd', jnp.square(jax.nn.relu(u)), w_down[l])

    return rmsnorm(x, final_norm_g)
```

```python
import math
from contextlib import ExitStack

import numpy as np
import concourse.bass as bass
import concourse.mybir as mybir
from concourse.bass_utils import run_bass_kernel_spmd

F32 = mybir.dt.float32
BF16 = mybir.dt.bfloat16
I32 = mybir.dt.int32
AF = mybir.ActivationFunctionType
ALU = mybir.AluOpType

SEQ = 4096
DM = 1024
NT = SEQ // 128
NG = SEQ // 512
NH = 8
HD = 64
IN_COLS = 2216
OFF_FQ, OFF_FK, OFF_FV, OFF_FF, OFF_CQ, OFF_CKV, OFF_KR = 0, 512, 1024, 1536, 1544, 1928, 2184
QL, KVL, ROPE = 384, 256, 32
DFF = 4096
NFC = DFF // 128
EPS = 1e-6
FOX_SCALE = 1.0 / math.sqrt(64)
MLA_SCALE = 1.0 / math.sqrt(96)
MASKVAL = -30000.0
INV_FREQ = [float(np.float32(10000.0) ** (-np.float32(i) / np.float32(32))) for i in range(0, 32, 2)]


class Op:
    __slots__ = ("eng", "fn", "deps", "dma", "needed", "val", "sem", "phase")

    def __init__(self, eng, fn, dma):
        self.eng = eng
        self.fn = fn
        self.dma = dma
        self.deps = []
        self.needed = False
        self.val = 0
        self.sem = None


class Sched:
    ENG = ("pe", "act", "dve", "pool", "sp")
    RING = {"sp": 8, "pool": 16, "act": 4}

    def __init__(self):
        self.streams = {e: [] for e in self.ENG}
        self.last_w = {}
        self.rd_c = {}
        self.rd_d = {}
        self.pending = {}
        self.phase = 'setup'

    @staticmethod
    def _name(k):
        return k[0] if isinstance(k, tuple) else k

    def _touch(self, k):
        if k not in self.last_w and k not in self.rd_c and k not in self.rd_d:
            p = self.pending.get(self._name(k))
            if p:
                self.rd_d[k] = list(p)

    def add(self, eng, fn, r=(), w=(), dma=False):
        o = Op(eng, fn, dma)
        o.phase = self.phase
        deps = {}

        def dep(d):
            if d is None:
                return
            if d.eng == "pe" and eng == "pe" and not d.dma and not dma:
                return
            deps[id(d)] = d

        for k in r:
            self._touch(k)
            dep(self.last_w.get(k))
        for k in w:
            self._touch(k)
            dep(self.last_w.get(k))
            for d in self.rd_c.get(k, {}).values():
                dep(d)
            for d in self.rd_d.get(k, ()):
                dep(d)
        for k in w:
            self.last_w[k] = o
            self.rd_c[k] = {}
            self.rd_d[k] = []
        for k in r:
            if k in w:
                continue
            if dma:
                self.rd_d.setdefault(k, []).append(o)
            else:
                self.rd_c.setdefault(k, {})[eng] = o
        o.deps = list(deps.values())
        self.streams[eng].append(o)
        return o

    def handoff(self, old_names, new_names):
        ops = {}
        old_names = set(old_names)
        for dct in (self.last_w,):
            for k in list(dct.keys()):
                if self._name(k) in old_names:
                    ops[id(dct[k])] = dct[k]
                    del dct[k]
        for k in list(self.rd_c.keys()):
            if self._name(k) in old_names:
                for d in self.rd_c[k].values():
                    ops[id(d)] = d
                del self.rd_c[k]
        for k in list(self.rd_d.keys()):
            if self._name(k) in old_names:
                for d in self.rd_d[k]:
                    ops[id(d)] = d
                del self.rd_d[k]
        for n in old_names:
            for d in self.pending.pop(n, []):
                ops[id(d)] = d
        for n in new_names:
            self.pending.setdefault(n, []).extend(ops.values())

    def emit(self, nc, es):
        sem = {e: es.enter_context(nc.semaphore("s_" + e)) for e in self.ENG}
        ring = {q: [es.enter_context(nc.semaphore("d_%s%d" % (q, i))) for i in range(n)]
                for q, n in self.RING.items()}
        for e in self.ENG:
            for o in self.streams[e]:
                for d in o.deps:
                    d.needed = True
        prev_ring = {}
        for e in self.ENG:
            cnt = 0
            dcnt = 0
            for o in self.streams[e]:
                if o.dma:
                    n = self.RING[e]
                    o.sem = ring[e][dcnt % n]
                    o.val = 16 * (dcnt // n + 1)
                    dcnt += 1
                elif o.needed:
                    cnt += 1
                    o.sem = sem[e]
                    o.val = cnt
        block = es.enter_context(nc.Block())
        streams = self.streams

        import os
        dump = os.environ.get('DUMP')
        logf = open(dump, 'w') if dump else None

        scopes = bool(os.environ.get('SCOPES'))

        def run(e, eng):
            waited = {}
            curp = None
            ctx = None
            for oi, o in enumerate(streams[e]):
                if scopes and o.phase != curp:
                    if ctx is not None:
                        ctx.__exit__(None, None, None)
                    ctx = nc.named_scope(o.phase)
                    ctx.__enter__()
                    curp = o.phase
                if logf:
                    logf.write("%s %d dma=%s sig=%s deps=%s\n" % (e, oi, o.dma, (o.sem.name, o.val) if o.sem else None,
                                                              sorted(set((d.sem.name, d.val, d.eng) for d in o.deps))))
                for d in o.deps:
                    if waited.get(d.sem, 0) < d.val:
                        eng.wait_ge(d.sem, d.val)
                        waited[d.sem] = d.val
                if o.dma and o.val > 16 and waited.get(o.sem, 0) < o.val - 16:
                    eng.wait_ge(o.sem, o.val - 16)
                    waited[o.sem] = o.val - 16
                if o.fn is None:
                    continue
                ins = o.fn(eng)
                if o.dma:
                    ins.then_inc(o.sem, 16)
                elif o.needed:
                    ins.then_inc(o.sem, 1)
            if ctx is not None:
                ctx.__exit__(None, None, None)

        block.tensor(lambda eng: run("pe", eng))
        block.scalar(lambda eng: run("act", eng))
        block.vector(lambda eng: run("dve", eng))
        block.gpsimd(lambda eng: run("pool", eng))
        block.sync(lambda eng: run("sp", eng))


class _Stop(Exception):
    pass


def build_program(debug=False, stop=None):
    nc = bass.Bass("TRN2", target_bir_lowering=False)
    S = Sched()

    def din(name, shape, dt=F32):
        return nc.dram_tensor(name, list(shape), dt, kind="ExternalInput").ap()

    x = din("x", [SEQ, DM])
    pos_t = din("pos_t", [128, NT], I32)
    attn_g = din("attn_norm_g", [DM])
    w_in = din("w_in", [DM, IN_COLS])
    b_forget = din("b_forget", [NH])
    q_g = din("q_norm_g", [QL])
    w_uq = din("w_uq", [QL, NH * 96])
    kv_g = din("kv_norm_g", [KVL])
    w_ukv = din("w_ukv", [KVL, NH * 128])
    fo_g = din("fox_out_g", [512])
    mo_g = din("mla_out_g", [512])
    w_o = din("w_o", [DM, DM])
    mlp_g = din("mlp_norm_g", [DM])
    w_up = din("w_up", [DM, DFF])
    w_down = din("w_down", [DFF, DM])
    fin_g = din("final_norm_g", [DM])
    c_ident = din("c_ident", [128, 128])
    c_tri = din("c_tri", [128, 128])
    c_mask = din("c_mask", [128, 128])
    out = nc.dram_tensor("out", [SEQ, DM], F32, kind="ExternalOutput").ap()
    wup_s = nc.dram_tensor("wup_s", [NFC, 128, 1024], BF16, kind="Internal").ap()
    wdn_s = nc.dram_tensor("wdn_s", [NFC, 128, 1024], BF16, kind="Internal").ap()
    if debug:
        dbg_o = nc.dram_tensor("dbg_o", [128, NT * 1024], BF16, kind="ExternalOutput").ap()
        dbg_f = nc.dram_tensor("dbg_f", [128, NT * 8], F32, kind="ExternalOutput").ap()

    SB_LO = 18432
    SB_HI = 229376
    cur = [SB_LO]
    peak = [0]

    def alloc(name, shape, dt, at=None):
        nbytes = int(np.prod(shape[1:])) * (4 if dt in (F32, I32) else 2)
        nbytes = (nbytes + 31) // 32 * 32
        if at is None:
            off = cur[0]
            cur[0] += nbytes
        else:
            off = at
        assert off + nbytes <= SB_HI, (name, off, nbytes)
        peak[0] = max(peak[0], off + nbytes)
        return nc.alloc_sbuf_tensor_at(name, list(shape), dt, offset=off), off + nbytes

    def A(name, shape, dt):
        return alloc(name, shape, dt)[0]

    identb = A("identb", [128, 128], BF16)
    maskb = A("maskb", [128, 128], BF16)
    COS_OFF = cur[0]
    cos_t = A("cos_t", [128, NT, 16], F32)
    SIN_OFF = cur[0]
    sin_t = A("sin_t", [128, NT, 16], F32)
    Ftab = A("Ftab", [128, NT, 8], F32)
    FS = A("FS", [128, NT, 8, 3], BF16)
    bfb = A("bfb", [128, 8], F32)
    st = A("st", [128, 24, NT], F32)
    gA = A("gA", [128, 1024], F32)
    rc = A("rc", [128, 2, 4], F32)
    ropet = A("ropet", [128, 4, 2, 16], F32)
    O_tok = A("O_tok", [128, NT, 1024], BF16)
    base = cur[0]

    hT, e1 = alloc("hT", [128, 8, SEQ], BF16, at=base)
    R2 = e1

    stack = ExitStack()
    S0 = stack.enter_context(nc.psum_tensor("S0", [128, 1024], F32))
    S1 = stack.enter_context(nc.psum_tensor("S1", [128, 1024], F32))
    OA0 = stack.enter_context(nc.psum_tensor("OA0", [128, 512], F32))
    OA1 = stack.enter_context(nc.psum_tensor("OA1", [128, 512], F32))
    P1 = stack.enter_context(nc.psum_tensor("P1", [128, 512], F32))
    PTf = stack.enter_context(nc.psum_tensor("PTf", [128, 512], F32))
    PT = PTf.bitcast(BF16)
    Sb = [S0, S1]
    OA = [OA0, OA1]

    def dma(q, out_ap, in_ap, r, w):
        return S.add(q, lambda e: e.dma_start(out=out_ap, in_=in_ap), r=r, w=w, dma=True)

    def act(out_ap, in_ap, func, r, w, **kw):
        return S.add("act", lambda e: e.activation(out=out_ap, in_=in_ap, func=func, **kw), r=r, w=w)

    def rstd_chain(ss_ap, tmp_ap, out_ap, n, keys_r, key_tmp, key_out):
        act(tmp_ap, ss_ap, AF.Ln, r=keys_r, w=[key_tmp], scale=1.0 / n, bias=EPS)
        act(out_ap, tmp_ap, AF.Exp, r=[key_tmp], w=[key_out], scale=-0.5)

    locals_ = {}
    xs = []
    hb = []
    o = R2
    NXS = 4
    for i in range(NXS):
        t_, o = alloc("xs%d" % i, [128, DM], F32, at=o)
        xs.append(t_)
    for i in range(2):
        t_, o = alloc("hb%d" % i, [128, DM], BF16, at=o)
        hb.append(t_)
    junk, o = alloc("junk", [128, DM], BF16, at=o)
    wff, o = alloc("wff", [128, 8, 8], BF16, at=o)
    for nm_, shp_, dt_ in (("stage", [128, 128], F32), ("trif", [128, 128], F32), ("onesf", [128, 128], F32),
                           ("posi", [128, NT], I32), ("posf", [128, NT], F32), ("angt", [128, NT, 16], F32),
                           ("zA", [128, NT, 8], F32), ("lfA", [128, NT, 8], F32), ("RnA", [128, NT + 1, 8], F32),
                           ("r1", [128, NT * 8], F32), ("r2", [128, NT * 8], F32),
                           ("ry", [128, NT * 16], F32), ("rki", [128, NT * 16], I32), ("rkf", [128, NT * 16], F32)):
        t_, o = alloc(nm_, shp_, dt_, at=o)
        locals_[nm_] = t_
    stage, trif, onesf, posi, posf, angt, zA, lfA, RnA, r1, r2, ry, rki, rkf = (locals_[k] for k in
        ("stage", "trif", "onesf", "posi", "posf", "angt", "zA", "lfA", "RnA", "r1", "r2", "ry", "rki", "rkf"))
    def finish():
        if debug:
            ok = [k for k in S.last_w if S._name(k) == "O_tok"]
            dma("sp", dbg_o, O_tok[:].rearrange("p t c -> p (t c)"), ok, [("dbg", 0)])
            dma("sp", dbg_f, Ftab[:].rearrange("p t c -> p (t c)"), [k for k in S.last_w if S._name(k) == "Ftab"], [("dbg", 1)])

        fin = S.add("sp", None, r=[("out", t) for t in range(NT)] + ([("dbg", 0), ("dbg", 1)] if debug else []), w=[])
        print('sbuf peak', peak[0], 'of', SB_HI, 'base', base, 'R2', R2)
        with stack:
            with ExitStack() as es:
                S.emit(nc, es)
        return nc

    dma("sp", stage[:], c_ident, [], ["stage"])
    S.add("dve", lambda e: e.tensor_copy(out=identb[:], in_=stage[:]), r=["stage"], w=["identb"])
    dma("sp", stage[:], c_mask, [], ["stage"])
    S.add("dve", lambda e: e.tensor_copy(out=maskb[:], in_=stage[:]), r=["stage"], w=["maskb"])
    dma("sp", trif[:], c_tri, [], ["trif"])
    S.add("pool", lambda e: e.memset(onesf[:], 1.0), w=["onesf"])
    S.add("pool", lambda e: e.memset(st[:], 0.0), w=["st0"])
    dma("sp", bfb[:], b_forget.partition_broadcast(128), [], ["bfb"])
    dma("sp", posi[:], pos_t, [], ["posi"])
    dma("sp", gA[:], attn_g.partition_broadcast(128), [], ["gA"])
    S.add("dve", lambda e: e.tensor_copy(out=posf[:], in_=posi[:]), r=["posi"], w=["posf"])
    for i in range(16):
        S.add("dve", lambda e, i=i: e.tensor_scalar(out=angt[:, :, i], in0=posf[:], scalar1=INV_FREQ[i], scalar2=0.0,
                                                    op0=ALU.mult, op1=ALU.add), r=["posf"], w=[("angt", i)])
    angk = [("angt", i) for i in range(16)]
    angf = angt[:].rearrange("p t i -> p (t i)")
    for (dst, shift, nm) in ((sin_t, 0.0, "sin_t"), (cos_t, 0.25, "cos_t")):
        dstf = dst[:].rearrange("p t i -> p (t i)")
        S.add("dve", lambda e, shift=shift: e.tensor_scalar(out=ry[:], in0=angf, scalar1=1.0 / (2.0 * math.pi), scalar2=shift,
                                                            op0=ALU.mult, op1=ALU.add), r=angk, w=["ry"])
        S.add("dve", lambda e: e.tensor_copy(out=rki[:], in_=ry[:]), r=["ry"], w=["rki"])
        S.add("dve", lambda e: e.tensor_copy(out=rkf[:], in_=rki[:]), r=["rki"], w=["rkf"])
        S.add("dve", lambda e: e.tensor_tensor(out=ry[:], in0=ry[:], in1=rkf[:], op=ALU.subtract), r=["ry", "rkf"], w=["ry"])
        S.add("dve", lambda e: e.tensor_scalar(out=rkf[:], in0=ry[:], scalar1=0.5, scalar2=1.0, op0=ALU.is_gt, op1=ALU.mult),
              r=["ry"], w=["rkf"])
        S.add("dve", lambda e: e.tensor_tensor(out=ry[:], in0=ry[:], in1=rkf[:], op=ALU.subtract), r=["ry", "rkf"], w=["ry"])
        S.add("dve", lambda e: e.tensor_scalar(out=rkf[:], in0=ry[:], scalar1=-0.5, scalar2=1.0, op0=ALU.is_lt, op1=ALU.mult),
              r=["ry"], w=["rkf"])
        S.add("dve", lambda e: e.tensor_tensor(out=ry[:], in0=ry[:], in1=rkf[:], op=ALU.add), r=["ry", "rkf"], w=["ry"])
        act(dstf, ry[:], AF.Sin, r=["ry"], w=[nm], scale=6.283184)

    dma("pool", wff[:], w_in.rearrange("(c p) f -> p c f", p=128)[:, :, OFF_FF:OFF_FF + 8], [], ["wff"])

    def norm_tile(src_ap, src_keys, gain, gain_key, dst_bf, dst_key, srow, col, n, junk_ap):
        ss = st[:, srow, col:col + 1]
        tmp = st[:, srow + 1, col:col + 1]
        rs = st[:, srow + 2, col:col + 1]
        kss, ktmp, krs = ("st", srow, col), ("st", srow + 1, col), ("st", srow + 2, col)
        act(junk_ap, src_ap, AF.Square, r=list(src_keys) + ["st0"], w=["junk", kss], accum_out=ss)
        rstd_chain(ss, tmp, rs, n, [kss], ktmp, krs)
        S.add("dve", lambda e: e.scalar_tensor_tensor(out=dst_bf, in0=src_ap, scalar=rs, in1=gain,
                                                      op0=ALU.mult, op1=ALU.mult),
              r=list(src_keys) + [krs, gain_key], w=[dst_key])
        return rs, krs

    def b1_tile(t):
        S.phase = 'B1'
        def mm(e, t=t):
            for c in range(8):
                ins = e.matmul(P1[:, 0:8], hT[:, c, t * 128:(t + 1) * 128], wff[:, c, :], start=(c == 0), stop=(c == 7))
            return ins
        S.add("pe", mm, r=[("hT", t), "wff"], w=[("P1", 0)])
        S.add("dve", lambda e, t=t: e.tensor_tensor(out=zA[:, t, :], in0=P1[:, 0:8], in1=bfb[:], op=ALU.add),
              r=[("P1", 0), "bfb"], w=[("zA", t)])
        act(zA[:, t, :], zA[:, t, :], AF.Exp, r=[("zA", t)], w=[("zA", t)], scale=-1.0)
        act(lfA[:, t, :], zA[:, t, :], AF.Ln, r=[("zA", t)], w=[("lfA", t)], bias=1.0)

        def mm2(e, t=t):
            e.matmul(OA0[:, 0:8], trif[:], lfA[:, t, :], start=True, stop=True)
            return e.matmul(OA0[:, 8:16], onesf[:], lfA[:, t, :], start=True, stop=True)
        S.add("pe", mm2, r=[("lfA", t), "trif", "onesf"], w=[("OA", 0)])
        S.add("dve", lambda e, t=t: e.scalar_tensor_tensor(out=Ftab[:, t, :], in0=OA0[:, 0:8], scalar=-1.0, in1=RnA[:, t, :],
                                                           op0=ALU.mult, op1=ALU.add),
              r=[("OA", 0), ("RnA", t)], w=[("Ftab", t)])
        S.add("dve", lambda e, t=t: e.scalar_tensor_tensor(out=RnA[:, t + 1, :], in0=OA0[:, 8:16], scalar=-1.0, in1=RnA[:, t, :],
                                                           op0=ALU.mult, op1=ALU.add),
              r=[("OA", 0), ("RnA", t)], w=[("RnA", t + 1)])
    S.add("pool", lambda e: e.memset(RnA[:, 0, :], 0.0), w=[("RnA", 0)])
    def A1(t):
        S.phase = 'A'
        b = t % 2
        xb_ = t % NXS
        dma("sp", xs[xb_][:], x[t * 128:(t + 1) * 128, :], [], [("xs", xb_)])
        norm_tile(xs[xb_][:], [("xs", xb_)], gA[:], "gA", hb[b][:], ("hb", b), 0, t, DM, junk[:])

    def A2(t):
        S.phase = 'A'
        b = t % 2

        def tr(e, b=b):
            for c in range(8):
                ins = e.transpose(PT[:, c * 128:(c + 1) * 128], hb[b][:, c * 128:(c + 1) * 128], identb[:])
            return ins
        S.add("pe", tr, r=[("hb", b), "identb"], w=["PT"])
        S.add("dve", lambda e, t=t: e.tensor_copy(out=hT[:, :, t * 128:(t + 1) * 128], in_=PT[:, :].rearrange("p (c f) -> p c f", c=8)),
              r=["PT"], w=[("hT", t)])

    A1(0)
    for t in range(NT):
        if t + 1 < NT:
            A1(t + 1)
        A2(t)
        if t >= 1:
            b1_tile(t - 1)
    b1_tile(NT - 1)
    if stop == 'A':
        return finish()
    Fk = [("Ftab", t) for t in range(NT)]
    Ff = Ftab[:].rearrange("p t h -> p (t h)")
    FSv = FS[:].rearrange("p t h s -> p (t h) s")
    S.add("dve", lambda e: e.tensor_copy(out=FSv[:, :, 0], in_=Ff), r=Fk, w=[("FS", 0)])
    S.add("dve", lambda e: e.tensor_tensor(out=r1[:], in0=Ff, in1=FSv[:, :, 0], op=ALU.subtract), r=Fk + [("FS", 0)], w=["r1"])
    S.add("dve", lambda e: e.tensor_copy(out=FSv[:, :, 1], in_=r1[:]), r=["r1"], w=[("FS", 1)])
    S.add("dve", lambda e: e.tensor_tensor(out=r2[:], in0=r1[:], in1=FSv[:, :, 1], op=ALU.subtract), r=["r1", ("FS", 1)], w=["r2"])
    S.add("dve", lambda e: e.tensor_copy(out=FSv[:, :, 2], in_=r2[:]), r=["r2"], w=[("FS", 2)])
    FSk = [("FS", 0), ("FS", 1), ("FS", 2)]

    if stop == 'B1':
        return finish()
    stash_list = []
    for fc in range(NFC):
        stash_list.append(("u", fc))
        stash_list.append(("d", fc))
    stash_pos = [0]

    def issue_stash(n):
        import os
        if os.environ.get('NOSTASH'):
            return
        for _ in range(n):
            if stash_pos[0] >= len(stash_list):
                return
            kind, fc = stash_list[stash_pos[0]]
            stash_pos[0] += 1
            if kind == "u":
                src = w_up.rearrange("(c p) (fc f) -> fc p c f", p=128, f=128)[fc]
                dst = wup_s[fc].rearrange("p (c f) -> p c f", c=8)
                dma("pool", dst, src, [], [("wup_s", fc)])
            else:
                dma("pool", wdn_s[fc], w_down[fc * 128:(fc + 1) * 128, :], [], [("wdn_s", fc)])

    class HB:
        pass

    def head_buffers(off, tag, kr_rows):
        hbuf = HB()
        o = off
        hbuf.QT, hbuf.KT, hbuf.V = [], [], []
        for i in range(2):
            t_, o = alloc("QT%s%d" % (tag, i), [128, SEQ], BF16, at=o)
            hbuf.QT.append(t_)
            t_, o = alloc("KT%s%d" % (tag, i), [128, SEQ], BF16, at=o)
            hbuf.KT.append(t_)
            t_, o = alloc("V%s%d" % (tag, i), [128, NT, 66], BF16, at=o)
            hbuf.V.append(t_)
        hbuf.PTb = []
        for i in range(3):
            t_, o = alloc("PTb%s%d" % (tag, i), [128, 1024], BF16, at=o)
            hbuf.PTb.append(t_)
        hbuf.OTs = []
        for i in range(2):
            t_, o = alloc("OTs%s%d" % (tag, i), [128, 512], BF16, at=o)
            hbuf.OTs.append(t_)
        hbuf.tag = tag
        hbuf.end = o
        hbuf.init = lambda: [S.add("pool", lambda e, i=i: e.memset(hbuf.OTs[i][:], 0.0), w=[("OTs" + tag, i)]) for i in range(2)]
        return hbuf

    def attention(hbuf, par, Kr, scale, ocol, inject=None, kt_extra=(), v_extra=()):
        tag = hbuf.tag
        tq = tag + str(par)
        aphase = tag + "_attn"
        S.phase = aphase
        QT, KT, V = hbuf.QT[par], hbuf.KT[par], hbuf.V[par]
        ucount = [0]
        pending_epi = [None]
        inject = list(inject or [])
        ninj = len(inject)
        injected = [0]
        NU_SPREAD = 84

        def do_inject(ui_global):
            target = min(ninj, ((ui_global + 1) * ninj + NU_SPREAD - 1) // NU_SPREAD)
            while injected[0] < target:
                inject[injected[0]]()
                injected[0] += 1
            S.phase = aphase

        def epilogue(g):
            oa = OA[g % 2]
            ots = hbuf.OTs[g % 2]
            S.add("dve", lambda e: e.tensor_copy(out=ots[0:65, :], in_=oa[0:65, :]), r=[("OA", g % 2)], w=[("OTs" + tag, g % 2)])

            def tr(e):
                for j in range(4):
                    ins = e.transpose(PT[:, j * 66:(j + 1) * 66], ots[0:66, j * 128:(j + 1) * 128], identb[0:66, 0:66])
                return ins
            S.add("pe", tr, r=[("OTs" + tag, g % 2), "identb"], w=["PT"])
            ptv = PT[:, 0:264].rearrange("p (j c) -> p j c", c=66)
            S.add("dve", lambda e: e.reciprocal(out=rc[:, g % 2, :], in_=ptv[:, :, 64]), r=["PT"], w=[("rc", g % 2)])
            for j in range(4):
                S.add("dve", lambda e, j=j: e.tensor_scalar(out=O_tok[:, 4 * g + j, ocol:ocol + 64], in0=ptv[:, j, 0:64],
                                                            scalar1=rc[:, g % 2, j:j + 1], scalar2=0.0, op0=ALU.mult, op1=ALU.add),
                      r=["PT", ("rc", g % 2)], w=[("O_tok", 4 * g + j, ocol)])

        for g in range(NG):
            units = []
            for kt in range(0, 4 * g, 2):
                units.append([(kt, 0), (kt + 1, 0)])
            for j in range(4):
                units.append([(4 * g + j, j * 128)])
            nun = len(units)
            q0 = g * 512
            oa = OA[g % 2]

            def qk(u, unit, q0=q0, g=g):
                sb = Sb[u % 2]
                diag = len(unit) == 1

                def f(e):
                    ins = None
                    for i, (kt, qlo) in enumerate(unit):
                        lhs = KT[0:Kr, kt * 128:(kt + 1) * 128]
                        if not diag:
                            ins = e.matmul(sb[:, i * 512:(i + 1) * 512], lhs, QT[0:Kr, q0:q0 + 512], start=True, stop=True)
                        else:
                            if qlo + 128 < 512:
                                e.matmul(sb[:, qlo + 128:512], lhs, QT[0:Kr, q0 + qlo + 128:q0 + 512], start=True, stop=True)
                            e.matmul(sb[:, qlo:qlo + 128], lhs, QT[0:Kr, q0 + qlo:q0 + qlo + 128], start=True, stop=False)
                            ins = e.matmul(sb[:, qlo:qlo + 128], identb[:], maskb[:], start=False, stop=True)
                    return ins
                S.add("pe", f, r=[("QT" + tq, g), ("KT" + tq,), "identb", "maskb"] + list(kt_extra), w=[("S", u % 2)])

            def ex(u, unit):
                sb = Sb[u % 2]
                pb = hbuf.PTb[u % 3]
                if len(unit) == 2:
                    act(pb[:, :], sb[:, :], AF.Exp, r=[("S", u % 2)], w=[("PTb" + tag, u % 3)], scale=scale)
                else:
                    qlo = unit[0][1]
                    act(pb[:, qlo:512], sb[:, qlo:512], AF.Exp, r=[("S", u % 2)], w=[("PTb" + tag, u % 3)], scale=scale)

            def pv(u, unit, first, last, oa=oa, g=g):
                pb = hbuf.PTb[u % 3]

                def f(e):
                    ins = None
                    for i, (kt, qlo) in enumerate(unit):
                        ins = e.matmul(oa[0:65, qlo:512], V[:, kt, 0:65], pb[:, i * 512 + qlo:(i + 1) * 512],
                                       start=(first and i == 0), stop=(last and i == len(unit) - 1))
                    return ins
                S.add("pe", f, r=[("PTb" + tag, u % 3), ("V" + tq,)] + list(v_extra), w=[("OA", g % 2)])

            for ui, unit in enumerate(units):
                u = ucount[0]
                ucount[0] += 1
                qk(u, unit)
                ex(u, unit)
                if ui >= 1:
                    pv(u - 1, units[ui - 1], ui - 1 == 0, False)
                if ui == 1 and pending_epi[0] is not None:
                    epilogue(pending_epi[0])
                    pending_epi[0] = None
                do_inject(u)
            pv(ucount[0] - 1, units[-1], nun == 1, True)
            pending_epi[0] = g
        epilogue(pending_epi[0])
        do_inject(10 ** 6)

    hbC = head_buffers(R2, "c", 70)
    o = hbC.end
    Wh = []
    for i in range(2):
        t_, o = alloc("Wh%d" % i, [128, 8, 192], BF16, at=o)
        Wh.append(t_)
    qtok = []
    ktok = []
    for i in range(2):
        t_, o = alloc("qtok%d" % i, [128, 2, 70], BF16, at=o)
        qtok.append(t_)
        t_, o = alloc("ktok%d" % i, [128, 2, 70], BF16, at=o)
        ktok.append(t_)
    CNAMES = ["QTc0", "QTc1", "KTc0", "KTc1", "Vc0", "Vc1", "PTbc", "OTsc", "Wh", "qtok", "ktok"]
    S.handoff(["xs", "hb", "junk", "wff", "stage", "trif", "onesf", "posi", "posf", "angt", "zA", "lfA", "RnA", "r1", "r2", "ry", "rki", "rkf"],
              CNAMES)
    for i in range(2):
        S.add("pool", lambda e, i=i: e.memset(qtok[i][:, :, 67:70], -8.0), w=[("qtok", i, "c")])
        S.add("pool", lambda e, i=i: e.memset(ktok[i][:, :, 64:67], 8.0), w=[("ktok", i, "c")])
        S.add("pool", lambda e, i=i: e.memset(hbC.V[i][:, :, 64:65], 1.0), w=[("Vc%d" % i, "ones")])
    hbC.init()
    w_in_v = w_in.rearrange("(c p) f -> p c f", p=128)

    def prep_c(h, par, overlap):
        QTp, KTp, Vp, Whp = hbC.QT[par], hbC.KT[par], hbC.V[par], Wh[par]
        tq = "c%d" % par
        Whk = [("Wh", par, 0), ("Wh", par, 1), ("Wh", par, 2)]
        L = []

        def wload():
            S.phase = 'c_prep'
            for k3, off3 in enumerate((OFF_FQ, OFF_FK, OFF_FV)):
                dma("pool", Whp[:, :, k3 * 64:(k3 + 1) * 64], w_in_v[:, :, off3 + h * 64: off3 + (h + 1) * 64], [], [("Wh", par, k3)])
            issue_stash(4)
        L.append(wload)
        for tp in range(NT // 2):
            b = tp % 2
            if overlap:
                PB, pbk = P1, [("P1", 0), ("P1", 1)]
            else:
                PB = (P1, S0)[tp % 2]
                pbk = ([("P1", 0), ("P1", 1)], [("S", 0)])[tp % 2]

            def mm_part(q, tp=tp, PB=PB, pbk=pbk):
                def step():
                    S.phase = 'c_prep'
                    j, c0 = q // 2, (q % 2) * 4

                    def mm(e):
                        t = 2 * tp + j
                        for c in range(c0, c0 + 4):
                            ins = e.matmul(PB[:, j * 192:(j + 1) * 192], hT[:, c, t * 128:(t + 1) * 128], Whp[:, c, :],
                                           start=(c == 0), stop=(c == 7))
                        return ins
                    S.add("pe", mm, r=[("hT", 2 * tp), ("hT", 2 * tp + 1)] + Whk, w=pbk)
                return step

            def mm_step(tp=tp, PB=PB, pbk=pbk, mm_part=mm_part):
                for q in range(4):
                    mm_part(q)()

            def ev_step(tp=tp, b=b, PB=PB, pbk=pbk):
                S.phase = 'c_prep'
                p1v = PB[:, 0:384].rearrange("p (j c) -> p j c", c=192)
                S.add("dve", lambda e: e.tensor_copy(out=qtok[b][:, :, 0:64], in_=p1v[:, :, 0:64]), r=pbk, w=[("qtok", b, "q")])
                S.add("dve", lambda e: e.tensor_copy(out=ktok[b][:, :, 0:64], in_=p1v[:, :, 64:128]), r=pbk, w=[("ktok", b, "k")])
                S.add("dve", lambda e: e.tensor_copy(out=Vp[:, 2 * tp:2 * tp + 2, 0:64], in_=p1v[:, :, 128:192]), r=pbk, w=[("V" + tq,)])
                S.add("pool", lambda e: e.tensor_copy(out=qtok[b][:, :, 64:67], in_=FS[:, 2 * tp:2 * tp + 2, h, :]),
                      r=FSk, w=[("qtok", b, "a")])
                S.add("pool", lambda e: e.tensor_copy(out=ktok[b][:, :, 67:70], in_=FS[:, 2 * tp:2 * tp + 2, h, :]),
                      r=FSk, w=[("ktok", b, "a")])

            def post_step(tp=tp, b=b):
                S.phase = 'c_prep'

                def tr(e):
                    for j in range(2):
                        e.transpose(PT[0:70, j * 128:(j + 1) * 128], qtok[b][:, j, :], identb[:])
                    for j in range(2):
                        ins = e.transpose(PT[0:70, 256 + j * 128:256 + (j + 1) * 128], ktok[b][:, j, :], identb[:])
                    return ins
                S.add("pe", tr, r=[("qtok", b, "q"), ("qtok", b, "a"), ("qtok", b, "c"), ("ktok", b, "k"), ("ktok", b, "a"),
                                   ("ktok", b, "c"), "identb"], w=["PT"])
                S.add("dve", lambda e: e.tensor_copy(out=QTp[0:70, tp * 256:(tp + 1) * 256], in_=PT[0:70, 0:256]),
                      r=["PT"], w=[("QT" + tq, tp // 2)])
                S.add("dve", lambda e: e.tensor_copy(out=KTp[0:70, tp * 256:(tp + 1) * 256], in_=PT[0:70, 256:512]),
                      r=["PT"], w=[("KT" + tq,)])
            if overlap:
                for q in range(4):
                    L.append(mm_part(q))
            else:
                L.append(mm_step)
            if tp >= 1:
                L.append(prev_post)
            L.append(ev_step)
            prev_post = post_step
        L.append(prev_post)
        return L

    for st_ in prep_c(0, 0, False):
        st_()
    if stop == 'Cp':
        return finish()
    for h in range(NH):
        nxt = prep_c(h + 1, (h + 1) % 2, True) if h + 1 < NH else []
        attention(hbC, h % 2, 70, FOX_SCALE, h * 64, inject=nxt, v_extra=[("Vc%d" % (h % 2), "ones")])
        if stop == 'C1':
            return finish()

    if stop == 'C':
        return finish()
    o = R2
    cqT, o = alloc("cqT", [128, 3, SEQ], BF16, at=o)
    ckvT, o = alloc("ckvT", [128, 2, SEQ], BF16, at=o)
    krT, o = alloc("krT", [128, SEQ], BF16, at=o)
    Wlat, o = alloc("Wlat", [128, 8, 672], BF16, at=o)
    cq_bs, ckv_bs, krpads = [], [], []
    for i in range(2):
        t_, o = alloc("cq_b%d" % i, [128, QL], BF16, at=o)
        cq_bs.append(t_)
        t_, o = alloc("ckv_b%d" % i, [128, KVL], BF16, at=o)
        ckv_bs.append(t_)
        t_, o = alloc("krpad%d" % i, [128, 96], BF16, at=o)
        krpads.append(t_)
    S.handoff(CNAMES, ["cqT", "ckvT", "krT", "Wlat", "cq_b", "ckv_b", "krpad"])
    S.phase = 'B2'
    dma("pool", Wlat[:], w_in_v[:, :, OFF_CQ:IN_COLS], [], ["Wlat"])
    dma("sp", gA[:, 0:QL], q_g.partition_broadcast(128), [], ["gA"])
    dma("sp", gA[:, QL:QL + KVL], kv_g.partition_broadcast(128), [], ["gA"])
    for i in range(2):
        S.add("pool", lambda e, i=i: e.memset(krpads[i][:, 0:64], 0.0), w=[("krpad", i, 0)])

    def rope(src_x1, src_x2, cosv, sinv, dst1, dst2, tmp, r, w, tmpkey):
        t0, t1_, t2_, t3_ = tmp
        S.add("dve", lambda e: e.tensor_tensor(out=t0, in0=src_x1, in1=cosv, op=ALU.mult), r=r, w=[(tmpkey, 0)])
        S.add("dve", lambda e: e.tensor_tensor(out=t1_, in0=src_x2, in1=sinv, op=ALU.mult), r=r, w=[(tmpkey, 1)])
        S.add("dve", lambda e: e.tensor_tensor(out=t2_, in0=src_x2, in1=cosv, op=ALU.mult), r=r, w=[(tmpkey, 2)])
        S.add("dve", lambda e: e.tensor_tensor(out=t3_, in0=src_x1, in1=sinv, op=ALU.mult), r=r, w=[(tmpkey, 3)])
        S.add("dve", lambda e: e.tensor_tensor(out=dst1, in0=t0, in1=t1_, op=ALU.subtract), r=[(tmpkey, 0), (tmpkey, 1)], w=w[0:1])
        S.add("dve", lambda e: e.tensor_tensor(out=dst2, in0=t2_, in1=t3_, op=ALU.add), r=[(tmpkey, 2), (tmpkey, 3)], w=w[1:2])

    junkq = gA[:, 640:1024].bitcast(BF16)

    def b2_mm(t):
        PQ = (P1, S0)[t % 2]
        PK = OA[t % 2]
        pqk = ([("P1", 0), ("P1", 1)], [("S", 0)])[t % 2]

        def mm(e, t=t, PQ=PQ, PK=PK):
            for c in range(8):
                e.matmul(PQ[:, 0:QL], hT[:, c, t * 128:(t + 1) * 128], Wlat[:, c, 0:QL], start=(c == 0), stop=(c == 7))
            for c in range(8):
                ins = e.matmul(PK[:, 0:288], hT[:, c, t * 128:(t + 1) * 128], Wlat[:, c, QL:QL + 288], start=(c == 0), stop=(c == 7))
            return ins
        S.add("pe", mm, r=[("hT", t), "Wlat"], w=pqk + [("OA", t % 2)])

    def b2_chain(t):
        tb = t % 2
        cq_b, ckv_b, krpad = cq_bs[tb], ckv_bs[tb], krpads[tb]
        PQ = (P1, S0)[t % 2]
        PK = OA[t % 2]
        pqk = ([("P1", 0), ("P1", 1)], [("S", 0)])[t % 2]
        okk = [("OA", t % 2)]
        for (src, n, gain, gk, dst, dk, srow, pk, jk) in (
                (PQ[:, 0:QL], QL, gA[:, 0:QL], "gA", cq_b[:], ("cq_b", tb), 3, pqk, junkq[:, 0:QL]),
                (PK[:, 0:KVL], KVL, gA[:, QL:QL + KVL], "gA", ckv_b[:], ("ckv_b", tb), 6, okk, junkq[:, QL:QL + KVL])):
            ss = st[:, srow, t:t + 1]
            tmp = st[:, srow + 1, t:t + 1]
            rs = st[:, srow + 2, t:t + 1]
            kss, ktmp, krs = ("st", srow, t), ("st", srow + 1, t), ("st", srow + 2, t)
            act(jk, src, AF.Square, r=pk + ["st0"], w=[("junkq", srow), kss], accum_out=ss)
            rstd_chain(ss, tmp, rs, n, [kss], ktmp, krs)
            S.add("dve", lambda e, dst=dst, src=src, rs=rs, gain=gain: e.scalar_tensor_tensor(
                out=dst, in0=src, scalar=rs, in1=gain, op0=ALU.mult, op1=ALU.mult), r=pk + [krs, gk], w=[dk])
        rope(PK[:, 256:272], PK[:, 272:288], cos_t[:, t, :], sin_t[:, t, :], krpad[:, 64:80], krpad[:, 80:96],
             [ropet[:, 0, 0, :], ropet[:, 1, 0, :], ropet[:, 2, 0, :], ropet[:, 3, 0, :]],
             r=okk + ["cos_t", "sin_t", ("st", 6, t)], w=[("krpad", tb, 1), ("krpad", tb, 2)], tmpkey="ropet")

    def b2_tr(t):
        tb = t % 2
        cq_b, ckv_b, krpad = cq_bs[tb], ckv_bs[tb], krpads[tb]

        def tr(e):
            for c in range(3):
                e.transpose(PT[:, c * 128:(c + 1) * 128], cq_b[:, c * 128:(c + 1) * 128], identb[:])
            for c in range(2):
                e.transpose(PT[:, 384 + c * 128:384 + (c + 1) * 128], ckv_b[:, c * 128:(c + 1) * 128], identb[:])
            return e.transpose(PT[0:96, 640:768], krpad[:, :], identb[:])
        S.add("pe", tr, r=[("cq_b", tb), ("ckv_b", tb), ("krpad", tb, 0), ("krpad", tb, 1), ("krpad", tb, 2), "identb"], w=["PT"])
        S.add("dve", lambda e, t=t: e.tensor_copy(out=cqT[:, :, t * 128:(t + 1) * 128], in_=PT[:, 0:384].rearrange("p (c f) -> p c f", c=3)),
              r=["PT"], w=[("cqT", t)])
        S.add("dve", lambda e, t=t: e.tensor_copy(out=ckvT[:, :, t * 128:(t + 1) * 128], in_=PT[:, 384:640].rearrange("p (c f) -> p c f", c=2)),
              r=["PT"], w=[("ckvT", t)])
        S.add("dve", lambda e, t=t: e.tensor_copy(out=krT[64:96, t * 128:(t + 1) * 128], in_=PT[64:96, 640:768]),
              r=["PT"], w=[("krT", t)])

    b2_mm(0)
    b2_chain(0)
    for t in range(NT):
        if t + 1 < NT:
            b2_mm(t + 1)
            b2_chain(t + 1)
        b2_tr(t)

    if stop == 'B2':
        return finish()
    hbD = head_buffers(base, "d", 96)
    o = hbD.end
    wuq, o = alloc("wuq", [128, 3, NH * 96], BF16, at=o)
    wukv, o = alloc("wukv", [128, 2, NH * 128], BF16, at=o)
    qtm = []
    ktm = []
    for i in range(2):
        t_, o = alloc("qtm%d" % i, [128, 2, 96], BF16, at=o)
        qtm.append(t_)
        t_, o = alloc("ktm%d" % i, [128, 2, 64], BF16, at=o)
        ktm.append(t_)
    assert o <= R2, (o, R2)
    DNAMES = ["QTd0", "QTd1", "KTd0", "KTd1", "Vd0", "Vd1", "PTbd", "OTsd", "wuq", "wukv", "qtm", "ktm"]
    S.handoff(["hT"], DNAMES)
    S.phase = 'd_prep'
    dma("pool", wuq[:], w_uq.rearrange("(c p) f -> p c f", p=128), [], ["wuq"])
    dma("pool", wukv[:], w_ukv.rearrange("(c p) f -> p c f", p=128), [], ["wukv"])
    hbD.init()
    for i in range(2):
        S.add("pool", lambda e, i=i: e.memset(hbD.V[i][:, :, 64:65], 1.0), w=[("Vd%d" % i, "ones")])
        S.add("pool", lambda e, i=i: e.tensor_copy(out=hbD.KT[i][64:96, :], in_=krT[64:96, :]),
              r=[("krT", t) for t in range(NT)], w=[("KTd%d" % i, "r")])

    def prep_d(h, par, overlap):
        QTp, KTp, Vp = hbD.QT[par], hbD.KT[par], hbD.V[par]
        tq = "d%d" % par
        L = []

        def wload():
            S.phase = 'd_prep'
            issue_stash(4)
        L.append(wload)
        for tp in range(NT // 2):
            b = tp % 2
            if overlap:
                PB, pbk = P1, [("P1", 0), ("P1", 1)]
            else:
                PB = (P1, S0)[tp % 2]
                pbk = ([("P1", 0), ("P1", 1)], [("S", 0)])[tp % 2]

            def mm_part(j, tp=tp, PB=PB, pbk=pbk):
                def step():
                    S.phase = 'd_prep'

                    def mm(e):
                        t = 2 * tp + j
                        for c in range(3):
                            e.matmul(PB[:, j * 96:(j + 1) * 96], cqT[:, c, t * 128:(t + 1) * 128], wuq[:, c, h * 96:(h + 1) * 96],
                                     start=(c == 0), stop=(c == 2))
                        for c in range(2):
                            ins = e.matmul(PB[:, 192 + j * 128:192 + (j + 1) * 128], ckvT[:, c, t * 128:(t + 1) * 128],
                                           wukv[:, c, h * 128:(h + 1) * 128], start=(c == 0), stop=(c == 1))
                        return ins
                    S.add("pe", mm, r=[("cqT", 2 * tp), ("cqT", 2 * tp + 1), ("ckvT", 2 * tp), ("ckvT", 2 * tp + 1), "wuq", "wukv"], w=pbk)
                return step

            def mm_step(tp=tp, PB=PB, pbk=pbk, mm_part=mm_part):
                for j in range(2):
                    mm_part(j)()

            def ev_step(tp=tp, b=b, PB=PB, pbk=pbk):
                S.phase = 'd_prep'
                pq = PB[:, 0:192].rearrange("p (j c) -> p j c", c=96)
                pkv = PB[:, 192:448].rearrange("p (j c) -> p j c", c=128)
                S.add("dve", lambda e: e.tensor_copy(out=qtm[b][:, :, 0:64], in_=pq[:, :, 0:64]), r=pbk, w=[("qtm", b, "n")])
                rope(pq[:, :, 64:80], pq[:, :, 80:96], cos_t[:, 2 * tp:2 * tp + 2, :], sin_t[:, 2 * tp:2 * tp + 2, :],
                     qtm[b][:, :, 64:80], qtm[b][:, :, 80:96],
                     [ropet[:, 0, :, :], ropet[:, 1, :, :], ropet[:, 2, :, :], ropet[:, 3, :, :]],
                     r=pbk + ["cos_t", "sin_t"], w=[("qtm", b, "r1"), ("qtm", b, "r2")], tmpkey="ropet")
                S.add("dve", lambda e: e.tensor_copy(out=ktm[b][:, :, :], in_=pkv[:, :, 0:64]), r=pbk, w=[("ktm", b)])
                S.add("dve", lambda e: e.tensor_copy(out=Vp[:, 2 * tp:2 * tp + 2, 0:64], in_=pkv[:, :, 64:128]), r=pbk, w=[("V" + tq,)])

            def post_step(tp=tp, b=b):
                S.phase = 'd_prep'

                def tr(e):
                    for j in range(2):
                        e.transpose(PT[0:96, j * 128:(j + 1) * 128], qtm[b][:, j, :], identb[:])
                    for j in range(2):
                        ins = e.transpose(PT[0:64, 256 + j * 128:256 + (j + 1) * 128], ktm[b][:, j, :], identb[:])
                    return ins
                S.add("pe", tr, r=[("qtm", b, "n"), ("qtm", b, "r1"), ("qtm", b, "r2"), ("ktm", b), "identb"], w=["PT"])
                S.add("dve", lambda e: e.tensor_copy(out=QTp[0:96, tp * 256:(tp + 1) * 256], in_=PT[0:96, 0:256]),
                      r=["PT"], w=[("QT" + tq, tp // 2)])
                S.add("dve", lambda e: e.tensor_copy(out=KTp[0:64, tp * 256:(tp + 1) * 256], in_=PT[0:64, 256:512]),
                      r=["PT"], w=[("KT" + tq,)])
            if overlap:
                for q in range(2):
                    L.append(mm_part(q))
            else:
                L.append(mm_step)
            if tp >= 1:
                L.append(prev_post)
            L.append(ev_step)
            prev_post = post_step
        L.append(prev_post)
        return L

    for st_ in prep_d(0, 0, False):
        st_()
    for h in range(NH):
        nxt = prep_d(h + 1, (h + 1) % 2, True) if h + 1 < NH else []
        attention(hbD, h % 2, 96, MLA_SCALE, 512 + h * 64, inject=nxt,
                  kt_extra=[("KTd%d" % (h % 2), "r")], v_extra=[("Vd%d" % (h % 2), "ones")])

    issue_stash(64)
    if stop == 'D':
        return finish()
    o = base
    wo_b, o = alloc("wo_b", [128, 8, DM], BF16, at=o)
    uT, o = alloc("uT", [128, NFC, 512], BF16, at=o)
    h2T, o = alloc("h2T", [128, 8, 512], BF16, at=o)
    mixb, mixT, h2b = [], [], []
    mixb_extra = [nc.alloc_sbuf_tensor_at("mixb2", [128, DM], BF16, offset=COS_OFF),
                  nc.alloc_sbuf_tensor_at("mixb3", [128, DM], BF16, offset=SIN_OFF)]
    for i in range(2):
        t_, o = alloc("mixb%d" % i, [128, DM], BF16, at=o)
        mixb.append(t_)
        t_, o = alloc("mixT%d" % i, [128, 8, 128], BF16, at=o)
        mixT.append(t_)
        t_, o = alloc("h2b%d" % i, [128, DM], BF16, at=o)
        h2b.append(t_)
    xin = []
    for i in range(2):
        t_, o = alloc("xin%d" % i, [128, DM], F32, at=o)
        xin.append(t_)
    x1s, o = alloc("x1s", [128, 4, DM], F32, at=o)
    NRING = 6
    wring = []
    for i in range(NRING):
        t_, o = alloc("wring%d" % i, [128, 1024], BF16, at=o)
        wring.append(t_)
    relu_t = []
    for i in range(2):
        t_, o = alloc("relu_t%d" % i, [128, 512], F32, at=o)
        relu_t.append(t_)
    outb = []
    for i in range(2):
        t_, o = alloc("outb%d" % i, [128, DM], F32, at=o)
        outb.append(t_)
    gC, o = alloc("gC", [128, DM], F32, at=o)
    gB, o = alloc("gB", [128, DM], F32, at=o)
    junkE, o = alloc("junkE", [128, DM], BF16, at=o)
    mixb = mixb + mixb_extra
    S.handoff(["cos_t", "sin_t"], ["mixb"])
    S.handoff(DNAMES + ["cqT", "ckvT", "krT", "Wlat", "cq_b", "ckv_b", "krpad"],
              ["wo_b", "uT", "h2T", "mixb", "mixT", "h2b", "xin", "x1s", "wring", "relu_t", "outb", "gC", "gB", "junkE"])
    dma("pool", wo_b[:], w_o.rearrange("(c p) f -> p c f", p=128), [], ["wo_b"])
    dma("sp", gA[:, 0:512], fo_g.partition_broadcast(128), [], ["gA"])
    dma("sp", gA[:, 512:1024], mo_g.partition_broadcast(128), [], ["gA"])
    dma("sp", gB[:], mlp_g.partition_broadcast(128), [], ["gB"])
    dma("sp", gC[:], fin_g.partition_broadcast(128), [], ["gC"])
    ACC = [PTf[:, :], P1[:, :], S0[:, 0:512], S0[:, 512:1024], S1[:, 0:512], S1[:, 512:1024], OA0[:, :], OA1[:, :]]
    ACCKL = [["PT"], [("P1", 0), ("P1", 1)], [("S", 0)], [("S", 0)], [("S", 1)], [("S", 1)], [("OA", 0)], [("OA", 1)]]

    def acck(i):
        return ACCKL[i]
    ring_i = [0]
    out_ops = []
    xloaded = set()

    def xload(t):
        if t in xloaded or t >= NT:
            return
        xloaded.add(t)
        dma("sp", xin[t % 2][:], x[t * 128:(t + 1) * 128, :], [], [("xin", t % 2)])

    def s1a(ps, j):
        if ps >= SEQ // 512:
            return
        S.phase = 'E_a'
        t = 4 * ps + j
        jb = j
        mb = mixb[j]
        okeys = [("O_tok", t, c * 64) for c in range(16)]
        for (lo, srow) in ((0, 9), (512, 12)):
            ss = st[:, srow, t:t + 1]
            tmp = st[:, srow + 1, t:t + 1]
            rs = st[:, srow + 2, t:t + 1]
            kss, ktmp, krs = ("st", srow, t), ("st", srow + 1, t), ("st", srow + 2, t)
            act(junkE[:, lo:lo + 512], O_tok[:, t, lo:lo + 512], AF.Square, r=okeys + ["st0"], w=[("junkE", lo), kss], accum_out=ss)
            rstd_chain(ss, tmp, rs, 512, [kss], ktmp, krs)
            S.add("dve", lambda e, lo=lo, t=t, rs=rs, mb=mb: e.scalar_tensor_tensor(
                out=mb[:, lo:lo + 512], in0=O_tok[:, t, lo:lo + 512], scalar=rs, in1=gA[:, lo:lo + 512],
                op0=ALU.mult, op1=ALU.mult), r=okeys + [krs, "gA"], w=[("mixb", jb, lo)])

    def s1b(ps, j):
        S.phase = 'E_a'
        jb = j % 2
        mb, mT = mixb[j], mixT[jb]
        SW = Sb[jb]

        def tr(e, mb=mb):
            for c in range(8):
                ins = e.transpose(PT[:, c * 128:(c + 1) * 128], mb[:, c * 128:(c + 1) * 128], identb[:])
            return ins
        S.add("pe", tr, r=[("mixb", j, 0), ("mixb", j, 512), "identb"], w=["PT"])
        act(mT[:, :, :], PT[:, :].rearrange("p (c f) -> p c f", c=8), AF.Copy, r=["PT"], w=[("mixT", jb)])

        def mmo(e, mT=mT, SW=SW):
            for half in range(2):
                for c in range(8):
                    ins = e.matmul(SW[:, half * 512:(half + 1) * 512], mT[:, c, :], wo_b[:, c, half * 512:(half + 1) * 512],
                                   start=(c == 0), stop=(c == 7))
            return ins
        S.add("pe", mmo, r=[("mixT", jb), "wo_b"], w=[("S", jb)])

    def s1c(ps, j):
        S.phase = 'E_a'
        t = 4 * ps + j
        jb = j % 2
        xb = xin[t % 2]
        hb2 = h2b[jb]
        SW = Sb[jb]
        xload(t)
        S.add("dve", lambda e, j=j, xb=xb, SW=SW: e.tensor_tensor(out=x1s[:, j, :], in0=SW[:, :], in1=xb[:], op=ALU.add),
              r=[("S", jb), ("xin", t % 2)], w=[("x1s", j)])
        xload(t + 2 if j < 2 else NT)
        ss = st[:, 15, t:t + 1]
        tmp = st[:, 16, t:t + 1]
        rs = st[:, 17, t:t + 1]
        kss, ktmp, krs = ("st", 15, t), ("st", 16, t), ("st", 17, t)
        act(junkE[:, :], x1s[:, j, :], AF.Square, r=[("x1s", j), "st0"], w=[("junkE", 0), ("junkE", 512), kss], accum_out=ss)
        rstd_chain(ss, tmp, rs, DM, [kss], ktmp, krs)
        S.add("dve", lambda e, j=j, rs=rs, hb2=hb2: e.scalar_tensor_tensor(out=hb2[:], in0=x1s[:, j, :], scalar=rs, in1=gB[:],
                                                                           op0=ALU.mult, op1=ALU.mult),
              r=[("x1s", j), krs, "gB"], w=[("h2b", jb)])

    def s2(ps, j):
        S.phase = 'E_a'
        jb = j % 2
        hb2 = h2b[jb]

        def tr2(e, hb2=hb2):
            for c in range(8):
                ins = e.transpose(PT[:, c * 128:(c + 1) * 128], hb2[:, c * 128:(c + 1) * 128], identb[:])
            return ins
        S.add("pe", tr2, r=[("h2b", jb), "identb"], w=["PT"])
        act(h2T[:, :, j * 128:(j + 1) * 128], PT[:, :].rearrange("p (c f) -> p c f", c=8), AF.Copy, r=["PT"], w=[("h2T", j)])

    def dtile(ps, j):
        if ps < 0:
            return
        S.phase = 'E_d'
        t = 4 * ps + j
        ob = outb[t % 2]
        for half in range(2):
            S.add("dve", lambda e, j=j, half=half: e.tensor_tensor(out=x1s[:, j, half * 512:(half + 1) * 512], in0=ACC[2 * j + half],
                                                                   in1=x1s[:, j, half * 512:(half + 1) * 512], op=ALU.add),
                  r=acck(2 * j + half) + [("x1s", j)], w=[("x1s", j)])
        ss = st[:, 18, t:t + 1]
        tmp = st[:, 19, t:t + 1]
        rs = st[:, 20, t:t + 1]
        kss, ktmp, krs = ("st", 18, t), ("st", 19, t), ("st", 20, t)
        act(junkE[:, :], x1s[:, j, :], AF.Square, r=[("x1s", j), "st0"], w=[("junkE", 0), ("junkE", 512), kss], accum_out=ss)
        rstd_chain(ss, tmp, rs, DM, [kss], ktmp, krs)
        S.add("dve", lambda e, j=j, rs=rs, ob=ob: e.scalar_tensor_tensor(out=ob[:], in0=x1s[:, j, :], scalar=rs, in1=gC[:],
                                                                        op0=ALU.mult, op1=ALU.mult),
              r=[("x1s", j), krs, "gC"], w=[("outb", t % 2)])
        out_ops.append(dma("pool", out[t * 128:(t + 1) * 128, :], ob[:], [("outb", t % 2)], [("out", t)]))

    NPS = SEQ // 512
    xload(0)
    xload(1)
    for j_ in range(4):
        s1a(0, j_)
    for ps in range(NPS):
        dtile(ps - 1, 0)
        dtile(ps - 1, 1)
        s1b(ps, 0)
        dtile(ps - 1, 2)
        s1c(ps, 0)
        s1b(ps, 1)
        dtile(ps - 1, 3)
        s2(ps, 0)
        s1c(ps, 1)
        s1b(ps, 2)
        s2(ps, 1)
        s1c(ps, 2)
        s1b(ps, 3)
        s2(ps, 2)
        s1c(ps, 3)
        s2(ps, 3)
        h2k = [("h2T", j) for j in range(4)]
        S.phase = 'E_b'
        for fc in range(NFC):
            ri = ring_i[0] % NRING
            ring_i[0] += 1
            wr = wring[ri]
            dma("sp", wr[:], wup_s[fc], [("wup_s", fc)], [("wring", ri)])
            ub = fc % 2
            Ub = (OA0, OA1)[ub]
            Ubk = ([("OA", 0)], [("OA", 1)])[ub]

            def mmu(e, wr=wr, Ub=Ub):
                for c in range(8):
                    ins = e.matmul(Ub[:, :], wr[:, c * 128:(c + 1) * 128], h2T[:, c, :], start=(c == 0), stop=(c == 7))
                return ins
            S.add("pe", mmu, r=[("wring", ri)] + h2k, w=Ubk)
            act(relu_t[ub][:], Ub[:, :], AF.Relu, r=Ubk, w=[("relu_t", ub)])
            S.add("pool", lambda e, ub=ub, fc=fc: e.tensor_tensor(out=uT[:, fc, :], in0=relu_t[ub][:], in1=relu_t[ub][:], op=ALU.mult),
                  r=[("relu_t", ub)], w=[("uT", fc)])
        xload(4 * (ps + 1))
        xload(4 * (ps + 1) + 1)
        for j_ in range(4):
            s1a(ps + 1, j_)
        S.phase = 'E_c'
        for fc in range(NFC):
            ri = ring_i[0] % NRING
            ring_i[0] += 1
            wr = wring[ri]
            dma("sp", wr[:], wdn_s[fc], [("wdn_s", fc)], [("wring", ri)])

            def mmd(e, wr=wr, fc=fc):
                for j in range(4):
                    for half in range(2):
                        ins = e.matmul(ACC[2 * j + half], uT[:, fc, j * 128:(j + 1) * 128], wr[:, half * 512:(half + 1) * 512],
                                       start=(fc == 0), stop=(fc == NFC - 1))
                return ins
            wk = []
            for i in range(8):
                wk += acck(i)
            S.add("pe", mmd, r=[("wring", ri), ("uT", fc)], w=list(dict.fromkeys(wk)))
    for j in range(4):
        dtile(NPS - 1, j)

    return finish()


_NC_CACHE = {}


def _consts():
    ident = np.eye(128, dtype=np.float32)
    tri = np.triu(np.ones((128, 128), dtype=np.float32))
    kk = np.arange(128)[:, None]
    qq = np.arange(128)[None, :]
    mask = np.where(kk <= qq, 0.0, MASKVAL).astype(np.float32)
    return ident, tri, mask


def kernel(x, positions, attn_norm_g, w_in, b_forget, q_norm_g, w_uq, kv_norm_g, w_ukv,
           fox_out_g, mla_out_g, w_o, mlp_norm_g, w_up, w_down, final_norm_g, _debug=False, _stop=None, _trace=False):
    key = (bool(_debug), _stop)
    if key not in _NC_CACHE:
        _NC_CACHE[key] = build_program(debug=bool(_debug), stop=_stop)
    nc = _NC_CACHE[key]
    ident, tri, mask = _consts()
    f = lambda a: np.ascontiguousarray(np.asarray(a, dtype=np.float32))
    x = np.asarray(x, dtype=np.float32)
    positions = np.asarray(positions, dtype=np.int32)
    B = x.shape[0]
    shared = dict(
        attn_norm_g=f(attn_norm_g).reshape(DM), w_in=f(w_in).reshape(DM, IN_COLS), b_forget=f(b_forget).reshape(NH),
        q_norm_g=f(q_norm_g).reshape(QL), w_uq=f(w_uq).reshape(QL, NH * 96), kv_norm_g=f(kv_norm_g).reshape(KVL),
        w_ukv=f(w_ukv).reshape(KVL, NH * 128), fox_out_g=f(fox_out_g).reshape(512), mla_out_g=f(mla_out_g).reshape(512),
        w_o=f(w_o).reshape(DM, DM), mlp_norm_g=f(mlp_norm_g).reshape(DM), w_up=f(w_up).reshape(DM, DFF),
        w_down=f(w_down).reshape(DFF, DM), final_norm_g=f(final_norm_g).reshape(DM),
        c_ident=ident, c_tri=tri, c_mask=mask,
    )
    in_maps = []
    for b in range(B):
        m = dict(shared)
        m["x"] = np.ascontiguousarray(x[b])
        m["pos_t"] = np.ascontiguousarray(positions[b].reshape(NT, 128).T)
        in_maps.append(m)
    res = run_bass_kernel_spmd(nc, in_maps, core_ids=list(range(B)), trace=True) if _trace else run_bass_kernel_spmd(nc, in_maps, core_ids=list(range(B)))
    outp = np.stack([np.asarray(r["out"], dtype=np.float32) for r in res.results], axis=0)
    if _debug or _trace:
        return outp, res
    return outp
```

```python
import math
from contextlib import ExitStack

import numpy as np
import concourse.bass as bass
import concourse.mybir as mybir
from concourse.bass_utils import run_bass_kernel_spmd

F32 = mybir.dt.float32
BF16 = mybir.dt.bfloat16
I32 = mybir.dt.int32
AF = mybir.ActivationFunctionType
ALU = mybir.AluOpType

SEQ = 4096
DM = 1024
NT = SEQ // 128
NG = SEQ // 512
NH = 8
HD = 64
IN_COLS = 2216
OFF_FQ, OFF_FK, OFF_FV, OFF_FF, OFF_CQ, OFF_CKV, OFF_KR = 0, 512, 1024, 1536, 1544, 1928, 2184
QL, KVL, ROPE = 384, 256, 32
DFF = 4096
NFC = DFF // 128
EPS = 1e-6
FOX_SCALE = 1.0 / math.sqrt(64)
MLA_SCALE = 1.0 / math.sqrt(96)
MASKVAL = -30000.0
INV_FREQ = [float(np.float32(10000.0) ** (-np.float32(i) / np.float32(32))) for i in range(0, 32, 2)]


class Op:
    __slots__ = ("eng", "fn", "deps", "dma", "needed", "val", "sem", "phase")

    def __init__(self, eng, fn, dma):
        self.eng = eng
        self.fn = fn
        self.dma = dma
        self.deps = []
        self.needed = False
        self.val = 0
        self.sem = None


class Sched:
    ENG = ("pe", "act", "dve", "pool", "sp")
    RING = {"sp": 8, "pool": 16, "act": 4}

    def __init__(self):
        self.streams = {e: [] for e in self.ENG}
        self.last_w = {}
        self.rd_c = {}
        self.rd_d = {}
        self.pending = {}
        self.phase = 'setup'

    @staticmethod
    def _name(k):
        return k[0] if isinstance(k, tuple) else k

    def _touch(self, k):
        if k not in self.last_w and k not in self.rd_c and k not in self.rd_d:
            p = self.pending.get(self._name(k))
            if p:
                self.rd_d[k] = list(p)

    def add(self, eng, fn, r=(), w=(), dma=False):
        o = Op(eng, fn, dma)
        o.phase = self.phase
        deps = {}

        def dep(d):
            if d is None:
                return
            if d.eng == "pe" and eng == "pe" and not d.dma and not dma:
                return
            deps[id(d)] = d

        for k in r:
            self._touch(k)
            dep(self.last_w.get(k))
        for k in w:
            self._touch(k)
            dep(self.last_w.get(k))
            for d in self.rd_c.get(k, {}).values():
                dep(d)
            for d in self.rd_d.get(k, ()):
                dep(d)
        for k in w:
            self.last_w[k] = o
            self.rd_c[k] = {}
            self.rd_d[k] = []
        for k in r:
            if k in w:
                continue
            if dma:
                self.rd_d.setdefault(k, []).append(o)
            else:
                self.rd_c.setdefault(k, {})[eng] = o
        o.deps = list(deps.values())
        self.streams[eng].append(o)
        return o

    def handoff(self, old_names, new_names):
        ops = {}
        old_names = set(old_names)
        for dct in (self.last_w,):
            for k in list(dct.keys()):
                if self._name(k) in old_names:
                    ops[id(dct[k])] = dct[k]
                    del dct[k]
        for k in list(self.rd_c.keys()):
            if self._name(k) in old_names:
                for d in self.rd_c[k].values():
                    ops[id(d)] = d
                del self.rd_c[k]
        for k in list(self.rd_d.keys()):
            if self._name(k) in old_names:
                for d in self.rd_d[k]:
                    ops[id(d)] = d
                del self.rd_d[k]
        for n in old_names:
            for d in self.pending.pop(n, []):
                ops[id(d)] = d
        for n in new_names:
            self.pending.setdefault(n, []).extend(ops.values())

    def emit(self, nc, es):
        sem = {e: es.enter_context(nc.semaphore("s_" + e)) for e in self.ENG}
        ring = {q: [es.enter_context(nc.semaphore("d_%s%d" % (q, i))) for i in range(n)]
                for q, n in self.RING.items()}
        for e in self.ENG:
            for o in self.streams[e]:
                for d in o.deps:
                    d.needed = True
        prev_ring = {}
        for e in self.ENG:
            cnt = 0
            dcnt = 0
            for o in self.streams[e]:
                if o.dma:
                    n = self.RING[e]
                    o.sem = ring[e][dcnt % n]
                    o.val = 16 * (dcnt // n + 1)
                    dcnt += 1
                elif o.needed:
                    cnt += 1
                    o.sem = sem[e]
                    o.val = cnt
        block = es.enter_context(nc.Block())
        streams = self.streams

        import os
        dump = os.environ.get('DUMP')
        logf = open(dump, 'w') if dump else None

        scopes = bool(os.environ.get('SCOPES'))

        def run(e, eng):
            waited = {}
            curp = None
            ctx = None
            for oi, o in enumerate(streams[e]):
                if scopes and o.phase != curp:
                    if ctx is not None:
                        ctx.__exit__(None, None, None)
                    ctx = nc.named_scope(o.phase)
                    ctx.__enter__()
                    curp = o.phase
                if logf:
                    logf.write("%s %d dma=%s sig=%s deps=%s\n" % (e, oi, o.dma, (o.sem.name, o.val) if o.sem else None,
                                                              sorted(set((d.sem.name, d.val, d.eng) for d in o.deps))))
                for d in o.deps:
                    if waited.get(d.sem, 0) < d.val:
                        eng.wait_ge(d.sem, d.val)
                        waited[d.sem] = d.val
                if o.dma and o.val > 16 and waited.get(o.sem, 0) < o.val - 16:
                    eng.wait_ge(o.sem, o.val - 16)
                    waited[o.sem] = o.val - 16
                if o.fn is None:
                    continue
                ins = o.fn(eng)
                if o.dma:
                    ins.then_inc(o.sem, 16)
                elif o.needed:
                    ins.then_inc(o.sem, 1)
            if ctx is not None:
                ctx.__exit__(None, None, None)

        block.tensor(lambda eng: run("pe", eng))
        block.scalar(lambda eng: run("act", eng))
        block.vector(lambda eng: run("dve", eng))
        block.gpsimd(lambda eng: run("pool", eng))
        block.sync(lambda eng: run("sp", eng))


class _Stop(Exception):
    pass


def build_program(debug=False, stop=None):
    nc = bass.Bass("TRN2", target_bir_lowering=False)
    S = Sched()

    def din(name, shape, dt=F32):
        return nc.dram_tensor(name, list(shape), dt, kind="ExternalInput").ap()

    x = din("x", [SEQ, DM])
    pos_t = din("pos_t", [128, NT], I32)
    attn_g = din("attn_norm_g", [DM])
    w_in = din("w_in", [DM, IN_COLS])
    b_forget = din("b_forget", [NH])
    q_g = din("q_norm_g", [QL])
    w_uq = din("w_uq", [QL, NH * 96])
    kv_g = din("kv_norm_g", [KVL])
    w_ukv = din("w_ukv", [KVL, NH * 128])
    fo_g = din("fox_out_g", [512])
    mo_g = din("mla_out_g", [512])
    w_o = din("w_o", [DM, DM])
    mlp_g = din("mlp_norm_g", [DM])
    w_up = din("w_up", [DM, DFF])
    w_down = din("w_down", [DFF, DM])
    fin_g = din("final_norm_g", [DM])
    c_ident = din("c_ident", [128, 128])
    c_tri = din("c_tri", [128, 128])
    c_mask = din("c_mask", [128, 128])
    out = nc.dram_tensor("out", [SEQ, DM], F32, kind="ExternalOutput").ap()
    wup_s = nc.dram_tensor("wup_s", [NFC, 128, 1024], BF16, kind="Internal").ap()
    wdn_s = nc.dram_tensor("wdn_s", [NFC, 128, 1024], BF16, kind="Internal").ap()
    if debug:
        dbg_o = nc.dram_tensor("dbg_o", [128, NT * 1024], BF16, kind="ExternalOutput").ap()
        dbg_f = nc.dram_tensor("dbg_f", [128, NT * 8], F32, kind="ExternalOutput").ap()

    SB_LO = 18432
    SB_HI = 229376
    cur = [SB_LO]
    peak = [0]

    def alloc(name, shape, dt, at=None):
        nbytes = int(np.prod(shape[1:])) * (4 if dt in (F32, I32) else 2)
        nbytes = (nbytes + 31) // 32 * 32
        if at is None:
            off = cur[0]
            cur[0] += nbytes
        else:
            off = at
        assert off + nbytes <= SB_HI, (name, off, nbytes)
        peak[0] = max(peak[0], off + nbytes)
        return nc.alloc_sbuf_tensor_at(name, list(shape), dt, offset=off), off + nbytes

    def A(name, shape, dt):
        return alloc(name, shape, dt)[0]

    identb = A("identb", [128, 128], BF16)
    maskb = A("maskb", [128, 128], BF16)
    COS_OFF = cur[0]
    cos_t = A("cos_t", [128, NT, 16], F32)
    SIN_OFF = cur[0]
    sin_t = A("sin_t", [128, NT, 16], F32)
    Ftab = A("Ftab", [128, NT, 8], F32)
    FS = A("FS", [128, NT, 8, 3], BF16)
    bfb = A("bfb", [128, 8], F32)
    st = A("st", [128, 24, NT], F32)
    gA = A("gA", [128, 1024], F32)
    rc = A("rc", [128, 2, 4], F32)
    ropet = A("ropet", [128, 4, 2, 16], F32)
    O_tok = A("O_tok", [128, NT, 1024], BF16)
    base = cur[0]

    hT, e1 = alloc("hT", [128, 8, SEQ], BF16, at=base)
    R2 = e1

    stack = ExitStack()
    S0 = stack.enter_context(nc.psum_tensor("S0", [128, 1024], F32))
    S1 = stack.enter_context(nc.psum_tensor("S1", [128, 1024], F32))
    OA0 = stack.enter_context(nc.psum_tensor("OA0", [128, 512], F32))
    OA1 = stack.enter_context(nc.psum_tensor("OA1", [128, 512], F32))
    P1 = stack.enter_context(nc.psum_tensor("P1", [128, 512], F32))
    PTf = stack.enter_context(nc.psum_tensor("PTf", [128, 512], F32))
    PT = PTf.bitcast(BF16)
    Sb = [S0, S1]
    OA = [OA0, OA1]

    def dma(q, out_ap, in_ap, r, w):
        return S.add(q, lambda e: e.dma_start(out=out_ap, in_=in_ap), r=r, w=w, dma=True)

    def act(out_ap, in_ap, func, r, w, **kw):
        return S.add("act", lambda e: e.activation(out=out_ap, in_=in_ap, func=func, **kw), r=r, w=w)

    def rstd_chain(ss_ap, tmp_ap, out_ap, n, keys_r, key_tmp, key_out):
        act(tmp_ap, ss_ap, AF.Ln, r=keys_r, w=[key_tmp], scale=1.0 / n, bias=EPS)
        act(out_ap, tmp_ap, AF.Exp, r=[key_tmp], w=[key_out], scale=-0.5)

    locals_ = {}
    xs = []
    hb = []
    o = R2
    NXS = 4
    for i in range(NXS):
        t_, o = alloc("xs%d" % i, [128, DM], F32, at=o)
        xs.append(t_)
    for i in range(2):
        t_, o = alloc("hb%d" % i, [128, DM], BF16, at=o)
        hb.append(t_)
    junk, o = alloc("junk", [128, DM], BF16, at=o)
    wff, o = alloc("wff", [128, 8, 8], BF16, at=o)
    for nm_, shp_, dt_ in (("stage", [128, 128], F32), ("trif", [128, 128], F32), ("onesf", [128, 128], F32),
                           ("posi", [128, NT], I32), ("posf", [128, NT], F32), ("angt", [128, NT, 16], F32),
                           ("zA", [128, NT, 8], F32), ("lfA", [128, NT, 8], F32), ("RnA", [128, NT + 1, 8], F32),
                           ("r1", [128, NT * 8], F32), ("r2", [128, NT * 8], F32),
                           ("ry", [128, NT * 16], F32), ("rki", [128, NT * 16], I32), ("rkf", [128, NT * 16], F32)):
        t_, o = alloc(nm_, shp_, dt_, at=o)
        locals_[nm_] = t_
    stage, trif, onesf, posi, posf, angt, zA, lfA, RnA, r1, r2, ry, rki, rkf = (locals_[k] for k in
        ("stage", "trif", "onesf", "posi", "posf", "angt", "zA", "lfA", "RnA", "r1", "r2", "ry", "rki", "rkf"))
    def finish():
        if debug:
            ok = [k for k in S.last_w if S._name(k) == "O_tok"]
            dma("sp", dbg_o, O_tok[:].rearrange("p t c -> p (t c)"), ok, [("dbg", 0)])
            dma("sp", dbg_f, Ftab[:].rearrange("p t c -> p (t c)"), [k for k in S.last_w if S._name(k) == "Ftab"], [("dbg", 1)])

        fin = S.add("sp", None, r=[("out", t) for t in range(NT)] + ([("dbg", 0), ("dbg", 1)] if debug else []), w=[])
        print('sbuf peak', peak[0], 'of', SB_HI, 'base', base, 'R2', R2)
        with stack:
            with ExitStack() as es:
                S.emit(nc, es)
        return nc

    dma("sp", stage[:], c_ident, [], ["stage"])
    S.add("dve", lambda e: e.tensor_copy(out=identb[:], in_=stage[:]), r=["stage"], w=["identb"])
    dma("sp", stage[:], c_mask, [], ["stage"])
    S.add("dve", lambda e: e.tensor_copy(out=maskb[:], in_=stage[:]), r=["stage"], w=["maskb"])
    dma("sp", trif[:], c_tri, [], ["trif"])
    S.add("pool", lambda e: e.memset(onesf[:], 1.0), w=["onesf"])
    S.add("pool", lambda e: e.memset(st[:], 0.0), w=["st0"])
    dma("sp", bfb[:], b_forget.partition_broadcast(128), [], ["bfb"])
    dma("sp", posi[:], pos_t, [], ["posi"])
    dma("sp", gA[:], attn_g.partition_broadcast(128), [], ["gA"])
    S.add("dve", lambda e: e.tensor_copy(out=posf[:], in_=posi[:]), r=["posi"], w=["posf"])
    for i in range(16):
        S.add("dve", lambda e, i=i: e.tensor_scalar(out=angt[:, :, i], in0=posf[:], scalar1=INV_FREQ[i], scalar2=0.0,
                                                    op0=ALU.mult, op1=ALU.add), r=["posf"], w=[("angt", i)])
    angk = [("angt", i) for i in range(16)]
    angf = angt[:].rearrange("p t i -> p (t i)")
    for (dst, shift, nm) in ((sin_t, 0.0, "sin_t"), (cos_t, 0.25, "cos_t")):
        dstf = dst[:].rearrange("p t i -> p (t i)")
        S.add("dve", lambda e, shift=shift: e.tensor_scalar(out=ry[:], in0=angf, scalar1=1.0 / (2.0 * math.pi), scalar2=shift,
                                                            op0=ALU.mult, op1=ALU.add), r=angk, w=["ry"])
        S.add("dve", lambda e: e.tensor_copy(out=rki[:], in_=ry[:]), r=["ry"], w=["rki"])
        S.add("dve", lambda e: e.tensor_copy(out=rkf[:], in_=rki[:]), r=["rki"], w=["rkf"])
        S.add("dve", lambda e: e.tensor_tensor(out=ry[:], in0=ry[:], in1=rkf[:], op=ALU.subtract), r=["ry", "rkf"], w=["ry"])
        S.add("dve", lambda e: e.tensor_scalar(out=rkf[:], in0=ry[:], scalar1=0.5, scalar2=1.0, op0=ALU.is_gt, op1=ALU.mult),
              r=["ry"], w=["rkf"])
        S.add("dve", lambda e: e.tensor_tensor(out=ry[:], in0=ry[:], in1=rkf[:], op=ALU.subtract), r=["ry", "rkf"], w=["ry"])
        S.add("dve", lambda e: e.tensor_scalar(out=rkf[:], in0=ry[:], scalar1=-0.5, scalar2=1.0, op0=ALU.is_lt, op1=ALU.mult),
              r=["ry"], w=["rkf"])
        S.add("dve", lambda e: e.tensor_tensor(out=ry[:], in0=ry[:], in1=rkf[:], op=ALU.add), r=["ry", "rkf"], w=["ry"])
        act(dstf, ry[:], AF.Sin, r=["ry"], w=[nm], scale=6.283184)

    dma("pool", wff[:], w_in.rearrange("(c p) f -> p c f", p=128)[:, :, OFF_FF:OFF_FF + 8], [], ["wff"])

    def norm_tile(src_ap, src_keys, gain, gain_key, dst_bf, dst_key, srow, col, n, junk_ap):
        ss = st[:, srow, col:col + 1]
        tmp = st[:, srow + 1, col:col + 1]
        rs = st[:, srow + 2, col:col + 1]
        kss, ktmp, krs = ("st", srow, col), ("st", srow + 1, col), ("st", srow + 2, col)
        act(junk_ap, src_ap, AF.Square, r=list(src_keys) + ["st0"], w=["junk", kss], accum_out=ss)
        rstd_chain(ss, tmp, rs, n, [kss], ktmp, krs)
        S.add("dve", lambda e: e.scalar_tensor_tensor(out=dst_bf, in0=src_ap, scalar=rs, in1=gain,
                                                      op0=ALU.mult, op1=ALU.mult),
              r=list(src_keys) + [krs, gain_key], w=[dst_key])
        return rs, krs

    def b1_tile(t):
        S.phase = 'B1'
        def mm(e, t=t):
            for c in range(8):
                ins = e.matmul(P1[:, 0:8], hT[:, c, t * 128:(t + 1) * 128], wff[:, c, :], start=(c == 0), stop=(c == 7))
            return ins
        S.add("pe", mm, r=[("hT", t), "wff"], w=[("P1", 0)])
        S.add("dve", lambda e, t=t: e.tensor_tensor(out=zA[:, t, :], in0=P1[:, 0:8], in1=bfb[:], op=ALU.add),
              r=[("P1", 0), "bfb"], w=[("zA", t)])
        act(zA[:, t, :], zA[:, t, :], AF.Exp, r=[("zA", t)], w=[("zA", t)], scale=-1.0)
        act(lfA[:, t, :], zA[:, t, :], AF.Ln, r=[("zA", t)], w=[("lfA", t)], bias=1.0)

        def mm2(e, t=t):
            e.matmul(OA0[:, 0:8], trif[:], lfA[:, t, :], start=True, stop=True)
            return e.matmul(OA0[:, 8:16], onesf[:], lfA[:, t, :], start=True, stop=True)
        S.add("pe", mm2, r=[("lfA", t), "trif", "onesf"], w=[("OA", 0)])
        S.add("dve", lambda e, t=t: e.scalar_tensor_tensor(out=Ftab[:, t, :], in0=OA0[:, 0:8], scalar=-1.0, in1=RnA[:, t, :],
                                                           op0=ALU.mult, op1=ALU.add),
              r=[("OA", 0), ("RnA", t)], w=[("Ftab", t)])
        S.add("dve", lambda e, t=t: e.scalar_tensor_tensor(out=RnA[:, t + 1, :], in0=OA0[:, 8:16], scalar=-1.0, in1=RnA[:, t, :],
                                                           op0=ALU.mult, op1=ALU.add),
              r=[("OA", 0), ("RnA", t)], w=[("RnA", t + 1)])
    S.add("pool", lambda e: e.memset(RnA[:, 0, :], 0.0), w=[("RnA", 0)])
    def A1(t):
        S.phase = 'A'
        b = t % 2
        xb_ = t % NXS
        dma("sp", xs[xb_][:], x[t * 128:(t + 1) * 128, :], [], [("xs", xb_)])
        norm_tile(xs[xb_][:], [("xs", xb_)], gA[:], "gA", hb[b][:], ("hb", b), 0, t, DM, junk[:])

    def A2(t):
        S.phase = 'A'
        b = t % 2

        def tr(e, b=b):
            for c in range(8):
                ins = e.transpose(PT[:, c * 128:(c + 1) * 128], hb[b][:, c * 128:(c + 1) * 128], identb[:])
            return ins
        S.add("pe", tr, r=[("hb", b), "identb"], w=["PT"])
        S.add("dve", lambda e, t=t: e.tensor_copy(out=hT[:, :, t * 128:(t + 1) * 128], in_=PT[:, :].rearrange("p (c f) -> p c f", c=8)),
              r=["PT"], w=[("hT", t)])

    A1(0)
    for t in range(NT):
        if t + 1 < NT:
            A1(t + 1)
        A2(t)
        if t >= 1:
            b1_tile(t - 1)
    b1_tile(NT - 1)
    if stop == 'A':
        return finish()
    Fk = [("Ftab", t) for t in range(NT)]
    Ff = Ftab[:].rearrange("p t h -> p (t h)")
    FSv = FS[:].rearrange("p t h s -> p (t h) s")
    S.add("dve", lambda e: e.tensor_copy(out=FSv[:, :, 0], in_=Ff), r=Fk, w=[("FS", 0)])
    S.add("dve", lambda e: e.tensor_tensor(out=r1[:], in0=Ff, in1=FSv[:, :, 0], op=ALU.subtract), r=Fk + [("FS", 0)], w=["r1"])
    S.add("dve", lambda e: e.tensor_copy(out=FSv[:, :, 1], in_=r1[:]), r=["r1"], w=[("FS", 1)])
    S.add("dve", lambda e: e.tensor_tensor(out=r2[:], in0=r1[:], in1=FSv[:, :, 1], op=ALU.subtract), r=["r1", ("FS", 1)], w=["r2"])
    S.add("dve", lambda e: e.tensor_copy(out=FSv[:, :, 2], in_=r2[:]), r=["r2"], w=[("FS", 2)])
    FSk = [("FS", 0), ("FS", 1), ("FS", 2)]

    if stop == 'B1':
        return finish()
    stash_list = []
    for fc in range(NFC):
        stash_list.append(("u", fc))
        stash_list.append(("d", fc))
    stash_pos = [0]

    def issue_stash(n):
        import os
        if os.environ.get('NOSTASH'):
            return
        for _ in range(n):
            if stash_pos[0] >= len(stash_list):
                return
            kind, fc = stash_list[stash_pos[0]]
            stash_pos[0] += 1
            if kind == "u":
                src = w_up.rearrange("(c p) (fc f) -> fc p c f", p=128, f=128)[fc]
                dst = wup_s[fc].rearrange("p (c f) -> p c f", c=8)
                dma("pool", dst, src, [], [("wup_s", fc)])
            else:
                dma("pool", wdn_s[fc], w_down[fc * 128:(fc + 1) * 128, :], [], [("wdn_s", fc)])

    class HB:
        pass

    def head_buffers(off, tag, kr_rows):
        hbuf = HB()
        o = off
        hbuf.QT, hbuf.KT, hbuf.V = [], [], []
        for i in range(2):
            t_, o = alloc("QT%s%d" % (tag, i), [128, SEQ], BF16, at=o)
            hbuf.QT.append(t_)
            t_, o = alloc("KT%s%d" % (tag, i), [128, SEQ], BF16, at=o)
            hbuf.KT.append(t_)
            t_, o = alloc("V%s%d" % (tag, i), [128, NT, 66], BF16, at=o)
            hbuf.V.append(t_)
        hbuf.PTb = []
        for i in range(3):
            t_, o = alloc("PTb%s%d" % (tag, i), [128, 1024], BF16, at=o)
            hbuf.PTb.append(t_)
        hbuf.OTs = []
        for i in range(2):
            t_, o = alloc("OTs%s%d" % (tag, i), [128, 512], BF16, at=o)
            hbuf.OTs.append(t_)
        hbuf.tag = tag
        hbuf.end = o
        hbuf.init = lambda: [S.add("pool", lambda e, i=i: e.memset(hbuf.OTs[i][:], 0.0), w=[("OTs" + tag, i)]) for i in range(2)]
        return hbuf

    def attention(hbuf, par, Kr, scale, ocol, inject=None, kt_extra=(), v_extra=()):
        tag = hbuf.tag
        tq = tag + str(par)
        aphase = tag + "_attn"
        S.phase = aphase
        QT, KT, V = hbuf.QT[par], hbuf.KT[par], hbuf.V[par]
        ucount = [0]
        pending_epi = [None]
        inject = list(inject or [])
        ninj = len(inject)
        injected = [0]
        NU_SPREAD = 84

        def do_inject(ui_global):
            target = min(ninj, ((ui_global + 1) * ninj + NU_SPREAD - 1) // NU_SPREAD)
            while injected[0] < target:
                inject[injected[0]]()
                injected[0] += 1
            S.phase = aphase

        def epilogue(g):
            oa = OA[g % 2]
            ots = hbuf.OTs[g % 2]
            S.add("dve", lambda e: e.tensor_copy(out=ots[0:65, :], in_=oa[0:65, :]), r=[("OA", g % 2)], w=[("OTs" + tag, g % 2)])

            def tr(e):
                for j in range(4):
                    ins = e.transpose(PT[:, j * 66:(j + 1) * 66], ots[0:66, j * 128:(j + 1) * 128], identb[0:66, 0:66])
                return ins
            S.add("pe", tr, r=[("OTs" + tag, g % 2), "identb"], w=["PT"])
            ptv = PT[:, 0:264].rearrange("p (j c) -> p j c", c=66)
            S.add("dve", lambda e: e.reciprocal(out=rc[:, g % 2, :], in_=ptv[:, :, 64]), r=["PT"], w=[("rc", g % 2)])
            for j in range(4):
                S.add("dve", lambda e, j=j: e.tensor_scalar(out=O_tok[:, 4 * g + j, ocol:ocol + 64], in0=ptv[:, j, 0:64],
                                                            scalar1=rc[:, g % 2, j:j + 1], scalar2=0.0, op0=ALU.mult, op1=ALU.add),
                      r=["PT", ("rc", g % 2)], w=[("O_tok", 4 * g + j, ocol)])

        for g in range(NG):
            units = []
            for kt in range(0, 4 * g, 2):
                units.append([(kt, 0), (kt + 1, 0)])
            for j in range(4):
                units.append([(4 * g + j, j * 128)])
            nun = len(units)
            q0 = g * 512
            oa = OA[g % 2]

            def qk(u, unit, q0=q0, g=g):
                sb = Sb[u % 2]
                diag = len(unit) == 1

                def f(e):
                    ins = None
                    for i, (kt, qlo) in enumerate(unit):
                        lhs = KT[0:Kr, kt * 128:(kt + 1) * 128]
                        if not diag:
                            ins = e.matmul(sb[:, i * 512:(i + 1) * 512], lhs, QT[0:Kr, q0:q0 + 512], start=True, stop=True)
                        else:
                            if qlo + 128 < 512:
                                e.matmul(sb[:, qlo + 128:512], lhs, QT[0:Kr, q0 + qlo + 128:q0 + 512], start=True, stop=True)
                            e.matmul(sb[:, qlo:qlo + 128], lhs, QT[0:Kr, q0 + qlo:q0 + qlo + 128], start=True, stop=False)
                            ins = e.matmul(sb[:, qlo:qlo + 128], identb[:], maskb[:], start=False, stop=True)
                    return ins
                S.add("pe", f, r=[("QT" + tq, g), ("KT" + tq,), "identb", "maskb"] + list(kt_extra), w=[("S", u % 2)])

            def ex(u, unit):
                sb = Sb[u % 2]
                pb = hbuf.PTb[u % 3]
                if len(unit) == 2:
                    act(pb[:, :], sb[:, :], AF.Exp, r=[("S", u % 2)], w=[("PTb" + tag, u % 3)], scale=scale)
                else:
                    qlo = unit[0][1]
                    act(pb[:, qlo:512], sb[:, qlo:512], AF.Exp, r=[("S", u % 2)], w=[("PTb" + tag, u % 3)], scale=scale)

            def pv(u, unit, first, last, oa=oa, g=g):
                pb = hbuf.PTb[u % 3]

                def f(e):
                    ins = None
                    for i, (kt, qlo) in enumerate(unit):
                        ins = e.matmul(oa[0:65, qlo:512], V[:, kt, 0:65], pb[:, i * 512 + qlo:(i + 1) * 512],
                                       start=(first and i == 0), stop=(last and i == len(unit) - 1))
                    return ins
                S.add("pe", f, r=[("PTb" + tag, u % 3), ("V" + tq,)] + list(v_extra), w=[("OA", g % 2)])

            for ui, unit in enumerate(units):
                u = ucount[0]
                ucount[0] += 1
                qk(u, unit)
                ex(u, unit)
                if ui >= 1:
                    pv(u - 1, units[ui - 1], ui - 1 == 0, False)
                if ui == 1 and pending_epi[0] is not None:
                    epilogue(pending_epi[0])
                    pending_epi[0] = None
                do_inject(u)
            pv(ucount[0] - 1, units[-1], nun == 1, True)
            pending_epi[0] = g
        epilogue(pending_epi[0])
        do_inject(10 ** 6)

    hbC = head_buffers(R2, "c", 70)
    o = hbC.end
    Wh = []
    for i in range(2):
        t_, o = alloc("Wh%d" % i, [128, 8, 192], BF16, at=o)
        Wh.append(t_)
    qtok = []
    ktok = []
    for i in range(2):
        t_, o = alloc("qtok%d" % i, [128, 2, 70], BF16, at=o)
        qtok.append(t_)
        t_, o = alloc("ktok%d" % i, [128, 2, 70], BF16, at=o)
        ktok.append(t_)
    CNAMES = ["QTc0", "QTc1", "KTc0", "KTc1", "Vc0", "Vc1", "PTbc", "OTsc", "Wh", "qtok", "ktok"]
    S.handoff(["xs", "hb", "junk", "wff", "stage", "trif", "onesf", "posi", "posf", "angt", "zA", "lfA", "RnA", "r1", "r2", "ry", "rki", "rkf"],
              CNAMES)
    for i in range(2):
        S.add("pool", lambda e, i=i: e.memset(qtok[i][:, :, 67:70], -8.0), w=[("qtok", i, "c")])
        S.add("pool", lambda e, i=i: e.memset(ktok[i][:, :, 64:67], 8.0), w=[("ktok", i, "c")])
        S.add("pool", lambda e, i=i: e.memset(hbC.V[i][:, :, 64:65], 1.0), w=[("Vc%d" % i, "ones")])
    hbC.init()
    w_in_v = w_in.rearrange("(c p) f -> p c f", p=128)

    def prep_c(h, par, overlap):
        QTp, KTp, Vp, Whp = hbC.QT[par], hbC.KT[par], hbC.V[par], Wh[par]
        tq = "c%d" % par
        Whk = [("Wh", par, 0), ("Wh", par, 1), ("Wh", par, 2)]
        L = []

        def wload():
            S.phase = 'c_prep'
            for k3, off3 in enumerate((OFF_FQ, OFF_FK, OFF_FV)):
                dma("pool", Whp[:, :, k3 * 64:(k3 + 1) * 64], w_in_v[:, :, off3 + h * 64: off3 + (h + 1) * 64], [], [("Wh", par, k3)])
            issue_stash(4)
        L.append(wload)
        for tp in range(NT // 2):
            b = tp % 2
            if overlap:
                PB, pbk = P1, [("P1", 0), ("P1", 1)]
            else:
                PB = (P1, S0)[tp % 2]
                pbk = ([("P1", 0), ("P1", 1)], [("S", 0)])[tp % 2]

            def mm_part(q, tp=tp, PB=PB, pbk=pbk):
                def step():
                    S.phase = 'c_prep'
                    j, c0 = q // 2, (q % 2) * 4

                    def mm(e):
                        t = 2 * tp + j
                        for c in range(c0, c0 + 4):
                            ins = e.matmul(PB[:, j * 192:(j + 1) * 192], hT[:, c, t * 128:(t + 1) * 128], Whp[:, c, :],
                                           start=(c == 0), stop=(c == 7))
                        return ins
                    S.add("pe", mm, r=[("hT", 2 * tp), ("hT", 2 * tp + 1)] + Whk, w=pbk)
                return step

            def mm_step(tp=tp, PB=PB, pbk=pbk, mm_part=mm_part):
                for q in range(4):
                    mm_part(q)()

            def ev_step(tp=tp, b=b, PB=PB, pbk=pbk):
                S.phase = 'c_prep'
                p1v = PB[:, 0:384].rearrange("p (j c) -> p j c", c=192)
                S.add("dve", lambda e: e.tensor_copy(out=qtok[b][:, :, 0:64], in_=p1v[:, :, 0:64]), r=pbk, w=[("qtok", b, "q")])
                S.add("dve", lambda e: e.tensor_copy(out=ktok[b][:, :, 0:64], in_=p1v[:, :, 64:128]), r=pbk, w=[("ktok", b, "k")])
                S.add("dve", lambda e: e.tensor_copy(out=Vp[:, 2 * tp:2 * tp + 2, 0:64], in_=p1v[:, :, 128:192]), r=pbk, w=[("V" + tq,)])
                S.add("pool", lambda e: e.tensor_copy(out=qtok[b][:, :, 64:67], in_=FS[:, 2 * tp:2 * tp + 2, h, :]),
                      r=FSk, w=[("qtok", b, "a")])
                S.add("pool", lambda e: e.tensor_copy(out=ktok[b][:, :, 67:70], in_=FS[:, 2 * tp:2 * tp + 2, h, :]),
                      r=FSk, w=[("ktok", b, "a")])

            def post_step(tp=tp, b=b):
                S.phase = 'c_prep'

                def tr(e):
                    for j in range(2):
                        e.transpose(PT[0:70, j * 128:(j + 1) * 128], qtok[b][:, j, :], identb[:])
                    for j in range(2):
                        ins = e.transpose(PT[0:70, 256 + j * 128:256 + (j + 1) * 128], ktok[b][:, j, :], identb[:])
                    return ins
                S.add("pe", tr, r=[("qtok", b, "q"), ("qtok", b, "a"), ("qtok", b, "c"), ("ktok", b, "k"), ("ktok", b, "a"),
                                   ("ktok", b, "c"), "identb"], w=["PT"])
                S.add("dve", lambda e: e.tensor_copy(out=QTp[0:70, tp * 256:(tp + 1) * 256], in_=PT[0:70, 0:256]),
                      r=["PT"], w=[("QT" + tq, tp // 2)])
                S.add("dve", lambda e: e.tensor_copy(out=KTp[0:70, tp * 256:(tp + 1) * 256], in_=PT[0:70, 256:512]),
                      r=["PT"], w=[("KT" + tq,)])
            if overlap:
                for q in range(4):
                    L.append(mm_part(q))
            else:
                L.append(mm_step)
            if tp >= 1:
                L.append(prev_post)
            L.append(ev_step)
            prev_post = post_step
        L.append(prev_post)
        return L

    for st_ in prep_c(0, 0, False):
        st_()
    if stop == 'Cp':
        return finish()
    for h in range(NH):
        nxt = prep_c(h + 1, (h + 1) % 2, True) if h + 1 < NH else []
        attention(hbC, h % 2, 70, FOX_SCALE, h * 64, inject=nxt, v_extra=[("Vc%d" % (h % 2), "ones")])
        if stop == 'C1':
            return finish()

    if stop == 'C':
        return finish()
    o = R2
    cqT, o = alloc("cqT", [128, 3, SEQ], BF16, at=o)
    ckvT, o = alloc("ckvT", [128, 2, SEQ], BF16, at=o)
    krT, o = alloc("krT", [128, SEQ], BF16, at=o)
    Wlat, o = alloc("Wlat", [128, 8, 672], BF16, at=o)
    cq_bs, ckv_bs, krpads = [], [], []
    for i in range(2):
        t_, o = alloc("cq_b%d" % i, [128, QL], BF16, at=o)
        cq_bs.append(t_)
        t_, o = alloc("ckv_b%d" % i, [128, KVL], BF16, at=o)
        ckv_bs.append(t_)
        t_, o = alloc("krpad%d" % i, [128, 96], BF16, at=o)
        krpads.append(t_)
    S.handoff(CNAMES, ["cqT", "ckvT", "krT", "Wlat", "cq_b", "ckv_b", "krpad"])
    S.phase = 'B2'
    dma("pool", Wlat[:], w_in_v[:, :, OFF_CQ:IN_COLS], [], ["Wlat"])
    dma("sp", gA[:, 0:QL], q_g.partition_broadcast(128), [], ["gA"])
    dma("sp", gA[:, QL:QL + KVL], kv_g.partition_broadcast(128), [], ["gA"])
    for i in range(2):
        S.add("pool", lambda e, i=i: e.memset(krpads[i][:, 0:64], 0.0), w=[("krpad", i, 0)])

    def rope(src_x1, src_x2, cosv, sinv, dst1, dst2, tmp, r, w, tmpkey):
        t0, t1_, t2_, t3_ = tmp
        S.add("dve", lambda e: e.tensor_tensor(out=t0, in0=src_x1, in1=cosv, op=ALU.mult), r=r, w=[(tmpkey, 0)])
        S.add("dve", lambda e: e.tensor_tensor(out=t1_, in0=src_x2, in1=sinv, op=ALU.mult), r=r, w=[(tmpkey, 1)])
        S.add("dve", lambda e: e.tensor_tensor(out=t2_, in0=src_x2, in1=cosv, op=ALU.mult), r=r, w=[(tmpkey, 2)])
        S.add("dve", lambda e: e.tensor_tensor(out=t3_, in0=src_x1, in1=sinv, op=ALU.mult), r=r, w=[(tmpkey, 3)])
        S.add("dve", lambda e: e.tensor_tensor(out=dst1, in0=t0, in1=t1_, op=ALU.subtract), r=[(tmpkey, 0), (tmpkey, 1)], w=w[0:1])
        S.add("dve", lambda e: e.tensor_tensor(out=dst2, in0=t2_, in1=t3_, op=ALU.add), r=[(tmpkey, 2), (tmpkey, 3)], w=w[1:2])

    junkq = gA[:, 640:1024].bitcast(BF16)

    def b2_mm(t):
        PQ = (P1, S0)[t % 2]
        PK = OA[t % 2]
        pqk = ([("P1", 0), ("P1", 1)], [("S", 0)])[t % 2]

        def mm(e, t=t, PQ=PQ, PK=PK):
            for c in range(8):
                e.matmul(PQ[:, 0:QL], hT[:, c, t * 128:(t + 1) * 128], Wlat[:, c, 0:QL], start=(c == 0), stop=(c == 7))
            for c in range(8):
                ins = e.matmul(PK[:, 0:288], hT[:, c, t * 128:(t + 1) * 128], Wlat[:, c, QL:QL + 288], start=(c == 0), stop=(c == 7))
            return ins
        S.add("pe", mm, r=[("hT", t), "Wlat"], w=pqk + [("OA", t % 2)])

    def b2_chain(t):
        tb = t % 2
        cq_b, ckv_b, krpad = cq_bs[tb], ckv_bs[tb], krpads[tb]
        PQ = (P1, S0)[t % 2]
        PK = OA[t % 2]
        pqk = ([("P1", 0), ("P1", 1)], [("S", 0)])[t % 2]
        okk = [("OA", t % 2)]
        for (src, n, gain, gk, dst, dk, srow, pk, jk) in (
                (PQ[:, 0:QL], QL, gA[:, 0:QL], "gA", cq_b[:], ("cq_b", tb), 3, pqk, junkq[:, 0:QL]),
                (PK[:, 0:KVL], KVL, gA[:, QL:QL + KVL], "gA", ckv_b[:], ("ckv_b", tb), 6, okk, junkq[:, QL:QL + KVL])):
            ss = st[:, srow, t:t + 1]
            tmp = st[:, srow + 1, t:t + 1]
            rs = st[:, srow + 2, t:t + 1]
            kss, ktmp, krs = ("st", srow, t), ("st", srow + 1, t), ("st", srow + 2, t)
            act(jk, src, AF.Square, r=pk + ["st0"], w=[("junkq", srow), kss], accum_out=ss)
            rstd_chain(ss, tmp, rs, n, [kss], ktmp, krs)
            S.add("dve", lambda e, dst=dst, src=src, rs=rs, gain=gain: e.scalar_tensor_tensor(
                out=dst, in0=src, scalar=rs, in1=gain, op0=ALU.mult, op1=ALU.mult), r=pk + [krs, gk], w=[dk])
        rope(PK[:, 256:272], PK[:, 272:288], cos_t[:, t, :], sin_t[:, t, :], krpad[:, 64:80], krpad[:, 80:96],
             [ropet[:, 0, 0, :], ropet[:, 1, 0, :], ropet[:, 2, 0, :], ropet[:, 3, 0, :]],
             r=okk + ["cos_t", "sin_t", ("st", 6, t)], w=[("krpad", tb, 1), ("krpad", tb, 2)], tmpkey="ropet")

    def b2_tr(t):
        tb = t % 2
        cq_b, ckv_b, krpad = cq_bs[tb], ckv_bs[tb], krpads[tb]

        def tr(e):
            for c in range(3):
                e.transpose(PT[:, c * 128:(c + 1) * 128], cq_b[:, c * 128:(c + 1) * 128], identb[:])
            for c in range(2):
                e.transpose(PT[:, 384 + c * 128:384 + (c + 1) * 128], ckv_b[:, c * 128:(c + 1) * 128], identb[:])
            return e.transpose(PT[0:96, 640:768], krpad[:, :], identb[:])
        S.add("pe", tr, r=[("cq_b", tb), ("ckv_b", tb), ("krpad", tb, 0), ("krpad", tb, 1), ("krpad", tb, 2), "identb"], w=["PT"])
        act(cqT[:, :, t * 128:(t + 1) * 128], PT[:, 0:384].rearrange("p (c f) -> p c f", c=3), AF.Copy, r=["PT"], w=[("cqT", t)])
        act(ckvT[:, :, t * 128:(t + 1) * 128], PT[:, 384:640].rearrange("p (c f) -> p c f", c=2), AF.Copy, r=["PT"], w=[("ckvT", t)])
        act(krT[64:96, t * 128:(t + 1) * 128], PT[64:96, 640:768], AF.Copy, r=["PT"], w=[("krT", t)])

    b2_mm(0)
    b2_chain(0)
    for t in range(NT):
        if t + 1 < NT:
            b2_mm(t + 1)
            b2_chain(t + 1)
        b2_tr(t)

    if stop == 'B2':
        return finish()
    hbD = head_buffers(base, "d", 96)
    o = hbD.end
    wuq, o = alloc("wuq", [128, 3, NH * 96], BF16, at=o)
    wukv, o = alloc("wukv", [128, 2, NH * 128], BF16, at=o)
    qtm = []
    ktm = []
    for i in range(2):
        t_, o = alloc("qtm%d" % i, [128, 2, 96], BF16, at=o)
        qtm.append(t_)
        t_, o = alloc("ktm%d" % i, [128, 2, 64], BF16, at=o)
        ktm.append(t_)
    assert o <= R2, (o, R2)
    DNAMES = ["QTd0", "QTd1", "KTd0", "KTd1", "Vd0", "Vd1", "PTbd", "OTsd", "wuq", "wukv", "qtm", "ktm"]
    S.handoff(["hT"], DNAMES)
    S.phase = 'd_prep'
    dma("pool", wuq[:], w_uq.rearrange("(c p) f -> p c f", p=128), [], ["wuq"])
    dma("pool", wukv[:], w_ukv.rearrange("(c p) f -> p c f", p=128), [], ["wukv"])
    hbD.init()
    for i in range(2):
        S.add("pool", lambda e, i=i: e.memset(hbD.V[i][:, :, 64:65], 1.0), w=[("Vd%d" % i, "ones")])
        S.add("pool", lambda e, i=i: e.tensor_copy(out=hbD.KT[i][64:96, :], in_=krT[64:96, :]),
              r=[("krT", t) for t in range(NT)], w=[("KTd%d" % i, "r")])

    def prep_d(h, par, overlap):
        QTp, KTp, Vp = hbD.QT[par], hbD.KT[par], hbD.V[par]
        tq = "d%d" % par
        L = []

        def wload():
            S.phase = 'd_prep'
            issue_stash(4)
        L.append(wload)
        for tp in range(NT // 2):
            b = tp % 2
            if overlap:
                PB, pbk = P1, [("P1", 0), ("P1", 1)]
            else:
                PB = (P1, S0)[tp % 2]
                pbk = ([("P1", 0), ("P1", 1)], [("S", 0)])[tp % 2]

            def mm_part(j, tp=tp, PB=PB, pbk=pbk):
                def step():
                    S.phase = 'd_prep'

                    def mm(e):
                        t = 2 * tp + j
                        for c in range(3):
                            e.matmul(PB[:, j * 96:(j + 1) * 96], cqT[:, c, t * 128:(t + 1) * 128], wuq[:, c, h * 96:(h + 1) * 96],
                                     start=(c == 0), stop=(c == 2))
                        for c in range(2):
                            ins = e.matmul(PB[:, 192 + j * 128:192 + (j + 1) * 128], ckvT[:, c, t * 128:(t + 1) * 128],
                                           wukv[:, c, h * 128:(h + 1) * 128], start=(c == 0), stop=(c == 1))
                        return ins
                    S.add("pe", mm, r=[("cqT", 2 * tp), ("cqT", 2 * tp + 1), ("ckvT", 2 * tp), ("ckvT", 2 * tp + 1), "wuq", "wukv"], w=pbk)
                return step

            def mm_step(tp=tp, PB=PB, pbk=pbk, mm_part=mm_part):
                for j in range(2):
                    mm_part(j)()

            def ev_step(tp=tp, b=b, PB=PB, pbk=pbk):
                S.phase = 'd_prep'
                pq = PB[:, 0:192].rearrange("p (j c) -> p j c", c=96)
                pkv = PB[:, 192:448].rearrange("p (j c) -> p j c", c=128)
                S.add("dve", lambda e: e.tensor_copy(out=qtm[b][:, :, 0:64], in_=pq[:, :, 0:64]), r=pbk, w=[("qtm", b, "n")])
                rope(pq[:, :, 64:80], pq[:, :, 80:96], cos_t[:, 2 * tp:2 * tp + 2, :], sin_t[:, 2 * tp:2 * tp + 2, :],
                     qtm[b][:, :, 64:80], qtm[b][:, :, 80:96],
                     [ropet[:, 0, :, :], ropet[:, 1, :, :], ropet[:, 2, :, :], ropet[:, 3, :, :]],
                     r=pbk + ["cos_t", "sin_t"], w=[("qtm", b, "r1"), ("qtm", b, "r2")], tmpkey="ropet")
                S.add("dve", lambda e: e.tensor_copy(out=ktm[b][:, :, :], in_=pkv[:, :, 0:64]), r=pbk, w=[("ktm", b)])
                S.add("dve", lambda e: e.tensor_copy(out=Vp[:, 2 * tp:2 * tp + 2, 0:64], in_=pkv[:, :, 64:128]), r=pbk, w=[("V" + tq,)])

            def post_step(tp=tp, b=b):
                S.phase = 'd_prep'

                def tr(e):
                    for j in range(2):
                        e.transpose(PT[0:96, j * 128:(j + 1) * 128], qtm[b][:, j, :], identb[:])
                    for j in range(2):
                        ins = e.transpose(PT[0:64, 256 + j * 128:256 + (j + 1) * 128], ktm[b][:, j, :], identb[:])
                    return ins
                S.add("pe", tr, r=[("qtm", b, "n"), ("qtm", b, "r1"), ("qtm", b, "r2"), ("ktm", b), "identb"], w=["PT"])
                S.add("dve", lambda e: e.tensor_copy(out=QTp[0:96, tp * 256:(tp + 1) * 256], in_=PT[0:96, 0:256]),
                      r=["PT"], w=[("QT" + tq, tp // 2)])
                S.add("dve", lambda e: e.tensor_copy(out=KTp[0:64, tp * 256:(tp + 1) * 256], in_=PT[0:64, 256:512]),
                      r=["PT"], w=[("KT" + tq,)])
            if overlap:
                for q in range(2):
                    L.append(mm_part(q))
            else:
                L.append(mm_step)
            if tp >= 1:
                L.append(prev_post)
            L.append(ev_step)
            prev_post = post_step
        L.append(prev_post)
        return L

    for st_ in prep_d(0, 0, False):
        st_()
    for h in range(NH):
        nxt = prep_d(h + 1, (h + 1) % 2, True) if h + 1 < NH else []
        attention(hbD, h % 2, 96, MLA_SCALE, 512 + h * 64, inject=nxt,
                  kt_extra=[("KTd%d" % (h % 2), "r")], v_extra=[("Vd%d" % (h % 2), "ones")])

    issue_stash(64)
    if stop == 'D':
        return finish()
    o = base
    wo_b, o = alloc("wo_b", [128, 8, DM], BF16, at=o)
    uT, o = alloc("uT", [128, NFC, 512], BF16, at=o)
    h2T, o = alloc("h2T", [128, 8, 512], BF16, at=o)
    mixb, mixT, h2b = [], [], []
    mixb_extra = [nc.alloc_sbuf_tensor_at("mixb2", [128, DM], BF16, offset=COS_OFF),
                  nc.alloc_sbuf_tensor_at("mixb3", [128, DM], BF16, offset=SIN_OFF)]
    for i in range(2):
        t_, o = alloc("mixb%d" % i, [128, DM], BF16, at=o)
        mixb.append(t_)
        t_, o = alloc("mixT%d" % i, [128, 8, 128], BF16, at=o)
        mixT.append(t_)
        t_, o = alloc("h2b%d" % i, [128, DM], BF16, at=o)
        h2b.append(t_)
    xin = []
    for i in range(2):
        t_, o = alloc("xin%d" % i, [128, DM], F32, at=o)
        xin.append(t_)
    x1s, o = alloc("x1s", [128, 4, DM], F32, at=o)
    NRING = 6
    wring = []
    for i in range(NRING):
        t_, o = alloc("wring%d" % i, [128, 1024], BF16, at=o)
        wring.append(t_)
    relu_t = []
    for i in range(2):
        t_, o = alloc("relu_t%d" % i, [128, 512], F32, at=o)
        relu_t.append(t_)
    outb = []
    for i in range(2):
        t_, o = alloc("outb%d" % i, [128, DM], F32, at=o)
        outb.append(t_)
    gC, o = alloc("gC", [128, DM], F32, at=o)
    gB, o = alloc("gB", [128, DM], F32, at=o)
    junkE, o = alloc("junkE", [128, DM], BF16, at=o)
    mixb = mixb + mixb_extra
    S.handoff(["cos_t", "sin_t"], ["mixb"])
    S.handoff(DNAMES + ["cqT", "ckvT", "krT", "Wlat", "cq_b", "ckv_b", "krpad"],
              ["wo_b", "uT", "h2T", "mixb", "mixT", "h2b", "xin", "x1s", "wring", "relu_t", "outb", "gC", "gB", "junkE"])
    dma("pool", wo_b[:], w_o.rearrange("(c p) f -> p c f", p=128), [], ["wo_b"])
    dma("sp", gA[:, 0:512], fo_g.partition_broadcast(128), [], ["gA"])
    dma("sp", gA[:, 512:1024], mo_g.partition_broadcast(128), [], ["gA"])
    dma("sp", gB[:], mlp_g.partition_broadcast(128), [], ["gB"])
    dma("sp", gC[:], fin_g.partition_broadcast(128), [], ["gC"])
    ACC = [PTf[:, :], P1[:, :], S0[:, 0:512], S0[:, 512:1024], S1[:, 0:512], S1[:, 512:1024], OA0[:, :], OA1[:, :]]
    ACCKL = [["PT"], [("P1", 0), ("P1", 1)], [("S", 0)], [("S", 0)], [("S", 1)], [("S", 1)], [("OA", 0)], [("OA", 1)]]

    def acck(i):
        return ACCKL[i]
    ring_i = [0]
    out_ops = []
    xloaded = set()

    def xload(t):
        if t in xloaded or t >= NT:
            return
        xloaded.add(t)
        dma("sp", xin[t % 2][:], x[t * 128:(t + 1) * 128, :], [], [("xin", t % 2)])

    def s1a(ps, j):
        if ps >= SEQ // 512:
            return
        S.phase = 'E_a'
        t = 4 * ps + j
        jb = j
        mb = mixb[j]
        okeys = [("O_tok", t, c * 64) for c in range(16)]
        for (lo, srow) in ((0, 9), (512, 12)):
            ss = st[:, srow, t:t + 1]
            tmp = st[:, srow + 1, t:t + 1]
            rs = st[:, srow + 2, t:t + 1]
            kss, ktmp, krs = ("st", srow, t), ("st", srow + 1, t), ("st", srow + 2, t)
            act(junkE[:, lo:lo + 512], O_tok[:, t, lo:lo + 512], AF.Square, r=okeys + ["st0"], w=[("junkE", lo), kss], accum_out=ss)
            rstd_chain(ss, tmp, rs, 512, [kss], ktmp, krs)
            S.add("dve", lambda e, lo=lo, t=t, rs=rs, mb=mb: e.scalar_tensor_tensor(
                out=mb[:, lo:lo + 512], in0=O_tok[:, t, lo:lo + 512], scalar=rs, in1=gA[:, lo:lo + 512],
                op0=ALU.mult, op1=ALU.mult), r=okeys + [krs, "gA"], w=[("mixb", jb, lo)])

    def s1b(ps, j):
        S.phase = 'E_a'
        jb = j % 2
        mb, mT = mixb[j], mixT[jb]
        SW = Sb[jb]

        def tr(e, mb=mb):
            for c in range(8):
                ins = e.transpose(PT[:, c * 128:(c + 1) * 128], mb[:, c * 128:(c + 1) * 128], identb[:])
            return ins
        S.add("pe", tr, r=[("mixb", j, 0), ("mixb", j, 512), "identb"], w=["PT"])
        act(mT[:, :, :], PT[:, :].rearrange("p (c f) -> p c f", c=8), AF.Copy, r=["PT"], w=[("mixT", jb)])

        def mmo(e, mT=mT, SW=SW):
            for half in range(2):
                for c in range(8):
                    ins = e.matmul(SW[:, half * 512:(half + 1) * 512], mT[:, c, :], wo_b[:, c, half * 512:(half + 1) * 512],
                                   start=(c == 0), stop=(c == 7))
            return ins
        S.add("pe", mmo, r=[("mixT", jb), "wo_b"], w=[("S", jb)])

    def s1c(ps, j):
        S.phase = 'E_a'
        t = 4 * ps + j
        jb = j % 2
        xb = xin[t % 2]
        hb2 = h2b[jb]
        SW = Sb[jb]
        xload(t)
        S.add("dve", lambda e, j=j, xb=xb, SW=SW: e.tensor_tensor(out=x1s[:, j, :], in0=SW[:, :], in1=xb[:], op=ALU.add),
              r=[("S", jb), ("xin", t % 2)], w=[("x1s", j)])
        xload(t + 2 if j < 2 else NT)
        ss = st[:, 15, t:t + 1]
        tmp = st[:, 16, t:t + 1]
        rs = st[:, 17, t:t + 1]
        kss, ktmp, krs = ("st", 15, t), ("st", 16, t), ("st", 17, t)
        act(junkE[:, :], x1s[:, j, :], AF.Square, r=[("x1s", j), "st0"], w=[("junkE", 0), ("junkE", 512), kss], accum_out=ss)
        rstd_chain(ss, tmp, rs, DM, [kss], ktmp, krs)
        S.add("dve", lambda e, j=j, rs=rs, hb2=hb2: e.scalar_tensor_tensor(out=hb2[:], in0=x1s[:, j, :], scalar=rs, in1=gB[:],
                                                                           op0=ALU.mult, op1=ALU.mult),
              r=[("x1s", j), krs, "gB"], w=[("h2b", jb)])

    def s2(ps, j):
        S.phase = 'E_a'
        jb = j % 2
        hb2 = h2b[jb]

        def tr2(e, hb2=hb2):
            for c in range(8):
                ins = e.transpose(PT[:, c * 128:(c + 1) * 128], hb2[:, c * 128:(c + 1) * 128], identb[:])
            return ins
        S.add("pe", tr2, r=[("h2b", jb), "identb"], w=["PT"])
        act(h2T[:, :, j * 128:(j + 1) * 128], PT[:, :].rearrange("p (c f) -> p c f", c=8), AF.Copy, r=["PT"], w=[("h2T", j)])

    def dtile(ps, j):
        if ps < 0:
            return
        S.phase = 'E_d'
        t = 4 * ps + j
        ob = outb[t % 2]
        for half in range(2):
            S.add("dve", lambda e, j=j, half=half: e.tensor_tensor(out=x1s[:, j, half * 512:(half + 1) * 512], in0=ACC[2 * j + half],
                                                                   in1=x1s[:, j, half * 512:(half + 1) * 512], op=ALU.add),
                  r=acck(2 * j + half) + [("x1s", j)], w=[("x1s", j)])
        ss = st[:, 18, t:t + 1]
        tmp = st[:, 19, t:t + 1]
        rs = st[:, 20, t:t + 1]
        kss, ktmp, krs = ("st", 18, t), ("st", 19, t), ("st", 20, t)
        act(junkE[:, :], x1s[:, j, :], AF.Square, r=[("x1s", j), "st0"], w=[("junkE", 0), ("junkE", 512), kss], accum_out=ss)
        rstd_chain(ss, tmp, rs, DM, [kss], ktmp, krs)
        S.add("dve", lambda e, j=j, rs=rs, ob=ob: e.scalar_tensor_tensor(out=ob[:], in0=x1s[:, j, :], scalar=rs, in1=gC[:],
                                                                        op0=ALU.mult, op1=ALU.mult),
              r=[("x1s", j), krs, "gC"], w=[("outb", t % 2)])
        out_ops.append(dma("pool", out[t * 128:(t + 1) * 128, :], ob[:], [("outb", t % 2)], [("out", t)]))

    NPS = SEQ // 512
    xload(0)
    xload(1)
    for j_ in range(4):
        s1a(0, j_)
    for ps in range(NPS):
        dtile(ps - 1, 0)
        dtile(ps - 1, 1)
        s1b(ps, 0)
        dtile(ps - 1, 2)
        s1c(ps, 0)
        s1b(ps, 1)
        dtile(ps - 1, 3)
        s2(ps, 0)
        s1c(ps, 1)
        s1b(ps, 2)
        s2(ps, 1)
        s1c(ps, 2)
        s1b(ps, 3)
        s2(ps, 2)
        s1c(ps, 3)
        s2(ps, 3)
        h2k = [("h2T", j) for j in range(4)]
        S.phase = 'E_b'
        for fc in range(NFC):
            ri = ring_i[0] % NRING
            ring_i[0] += 1
            wr = wring[ri]
            dma("sp", wr[:], wup_s[fc], [("wup_s", fc)], [("wring", ri)])
            ub = fc % 2
            Ub = (OA0, OA1)[ub]
            Ubk = ([("OA", 0)], [("OA", 1)])[ub]

            def mmu(e, wr=wr, Ub=Ub):
                for c in range(8):
                    ins = e.matmul(Ub[:, :], wr[:, c * 128:(c + 1) * 128], h2T[:, c, :], start=(c == 0), stop=(c == 7))
                return ins
            S.add("pe", mmu, r=[("wring", ri)] + h2k, w=Ubk)
            act(relu_t[ub][:], Ub[:, :], AF.Relu, r=Ubk, w=[("relu_t", ub)])
            S.add("pool", lambda e, ub=ub, fc=fc: e.tensor_tensor(out=uT[:, fc, :], in0=relu_t[ub][:], in1=relu_t[ub][:], op=ALU.mult),
                  r=[("relu_t", ub)], w=[("uT", fc)])
        xload(4 * (ps + 1))
        xload(4 * (ps + 1) + 1)
        for j_ in range(4):
            s1a(ps + 1, j_)
        S.phase = 'E_c'
        for fc in range(NFC):
            ri = ring_i[0] % NRING
            ring_i[0] += 1
            wr = wring[ri]
            dma("sp", wr[:], wdn_s[fc], [("wdn_s", fc)], [("wring", ri)])

            def mmd(e, wr=wr, fc=fc):
                for j in range(4):
                    for half in range(2):
                        ins = e.matmul(ACC[2 * j + half], uT[:, fc, j * 128:(j + 1) * 128], wr[:, half * 512:(half + 1) * 512],
                                       start=(fc == 0), stop=(fc == NFC - 1))
                return ins
            wk = []
            for i in range(8):
                wk += acck(i)
            S.add("pe", mmd, r=[("wring", ri), ("uT", fc)], w=list(dict.fromkeys(wk)))
    for j in range(4):
        dtile(NPS - 1, j)

    return finish()


_NC_CACHE = {}


def _consts():
    ident = np.eye(128, dtype=np.float32)
    tri = np.triu(np.ones((128, 128), dtype=np.float32))
    kk = np.arange(128)[:, None]
    qq = np.arange(128)[None, :]
    mask = np.where(kk <= qq, 0.0, MASKVAL).astype(np.float32)
    return ident, tri, mask


def kernel(x, positions, attn_norm_g, w_in, b_forget, q_norm_g, w_uq, kv_norm_g, w_ukv,
           fox_out_g, mla_out_g, w_o, mlp_norm_g, w_up, w_down, final_norm_g, _debug=False, _stop=None, _trace=False):
    key = (bool(_debug), _stop)
    if key not in _NC_CACHE:
        _NC_CACHE[key] = build_program(debug=bool(_debug), stop=_stop)
    nc = _NC_CACHE[key]
    ident, tri, mask = _consts()
    f = lambda a: np.ascontiguousarray(np.asarray(a, dtype=np.float32))
    x = np.asarray(x, dtype=np.float32)
    positions = np.asarray(positions, dtype=np.int32)
    B = x.shape[0]
    shared = dict(
        attn_norm_g=f(attn_norm_g).reshape(DM), w_in=f(w_in).reshape(DM, IN_COLS), b_forget=f(b_forget).reshape(NH),
        q_norm_g=f(q_norm_g).reshape(QL), w_uq=f(w_uq).reshape(QL, NH * 96), kv_norm_g=f(kv_norm_g).reshape(KVL),
        w_ukv=f(w_ukv).reshape(KVL, NH * 128), fox_out_g=f(fox_out_g).reshape(512), mla_out_g=f(mla_out_g).reshape(512),
        w_o=f(w_o).reshape(DM, DM), mlp_norm_g=f(mlp_norm_g).reshape(DM), w_up=f(w_up).reshape(DM, DFF),
        w_down=f(w_down).reshape(DFF, DM), final_norm_g=f(final_norm_g).reshape(DM),
        c_ident=ident, c_tri=tri, c_mask=mask,
    )
    in_maps = []
    for b in range(B):
        m = dict(shared)
        m["x"] = np.ascontiguousarray(x[b])
        m["pos_t"] = np.ascontiguousarray(positions[b].reshape(NT, 128).T)
        in_maps.append(m)
    res = run_bass_kernel_spmd(nc, in_maps, core_ids=list(range(B)), trace=True) if _trace else run_bass_kernel_spmd(nc, in_maps, core_ids=list(range(B)))
    outp = np.stack([np.asarray(r["out"], dtype=np.float32) for r in res.results], axis=0)
    if _debug or _trace:
        return outp, res
    return outp
```
